# Optimizing a Trainium2 kernel written in Bass

```python
import jax
import jax.numpy as jnp
from jax import lax

D_MODEL = 1024
BATCH = 8
SEQ = 4096
DEPTH = 4

GRID_W = 64
CTX_LEN = 256
EPS = 1e-6

GLA_HEADS = 4
GLA_DK = 64
GLA_DV = 128
GLA_QK = GLA_HEADS * GLA_DK
GLA_V = GLA_HEADS * GLA_DV
GLA_RANK = 16
GLA_TAU = 16.0
GLA_CHUNK = 64

FNET_GROUPS = 4
FNET_GD = 64
FNET_W = FNET_GROUPS * FNET_GD

RG_HEADS = 4
RG_HD = 64
RG_W = RG_HEADS * RG_HD
RG_C = 8.0
CONV_W = 4
CONV_LEFT = 1

MIX_W = GLA_V + FNET_W + RG_W
D_FF = -(-8 * D_MODEL // (3 * 256)) * 256

OFF_K = GLA_QK
OFF_V = OFF_K + GLA_QK
OFF_DEC = OFF_V + GLA_V
OFF_OG = OFF_DEC + 2 * GLA_RANK
OFF_F = OFF_OG + GLA_V
OFF_RX = OFF_F + FNET_W
OFF_RG = OFF_RX + RG_W
N_IN = OFF_RG + RG_W
IN_SPLITS = (OFF_K, OFF_V, OFF_DEC, OFF_OG, OFF_F, OFF_RX, OFF_RG)

kernel_name = 'hybrid_gla_fnet_rglru_prefix_dit'


def rmsnorm(x, g):
    xf = x.astype(jnp.float32)
    y = xf * lax.rsqrt(jnp.mean(xf * xf, axis=-1, keepdims=True) + EPS)
    return (y * g.astype(jnp.float32)).astype(x.dtype)


def modulate(h, shift, scale):
    return h * (1 + scale) + shift


def rev(t):
    return jnp.flip(t, axis=1)


def swiglu(h, w_gate, w_up, w_down):
    return (jax.nn.silu(h @ w_gate) * (h @ w_up)) @ w_down


def gla_prepare(z_q, z_k, z_v, z_dec, w_dec, b_dec):
    B, T, _ = z_q.shape
    f32 = jnp.float32
    q = z_q.astype(f32).reshape(B, T, GLA_HEADS, GLA_DK) * (GLA_DK ** -0.5)
    k = z_k.astype(f32).reshape(B, T, GLA_HEADS, GLA_DK)
    v = z_v.astype(f32).reshape(B, T, GLA_HEADS, GLA_DV)
    lr = z_dec.astype(f32).reshape(B, T, 2, GLA_RANK)
    logit = jnp.einsum('btdr,drk->btdk', lr, w_dec.astype(f32)) + b_dec.astype(f32)
    log_a = (jax.nn.log_sigmoid(logit) / GLA_TAU).reshape(B, T, 2, GLA_HEADS, GLA_DK)
    return q, k, v, log_a[:, :, 0], log_a[:, :, 1]


def gla_scan(q, k, v, log_a, s0):
    B, T, H, _ = q.shape
    N = T // GLA_CHUNK

    def chunks(t):
        return t.reshape(B, N, GLA_CHUNK, H, t.shape[-1]).transpose(1, 0, 3, 2, 4)

    qc, kc, vc, gc = chunks(q), chunks(k), chunks(v), chunks(log_a)
    b = jnp.cumsum(gc, axis=3)
    b_last = b[:, :, :, -1:, :]
    q_t = qc * jnp.exp(b)
    k_t = kc * jnp.exp(-b)
    k_end = kc * jnp.exp(b_last - b)
    mask = jnp.tril(jnp.ones((GLA_CHUNK, GLA_CHUNK), dtype=bool))
    attn = jnp.where(mask, jnp.einsum('nbhik,nbhjk->nbhij', q_t, k_t), 0.0)
    o_intra = jnp.einsum('nbhij,nbhjv->nbhiv', attn, vc)

    def step(s, inp):
        q_n, k_n, v_n, d_n = inp
        o_n = jnp.einsum('bhik,bhkv->bhiv', q_n, s)
        s_new = s * d_n[:, :, 0, :, None] + jnp.einsum('bhjk,bhjv->bhkv', k_n, v_n)
        return s_new, o_n

    s_fin, o_inter = lax.scan(step, s0, (q_t, k_end, vc, jnp.exp(b_last)))
    o = (o_intra + o_inter).transpose(1, 0, 3, 2, 4).reshape(B, T, H, v.shape[-1])
    return o, s_fin


def gla_final_state(k, v, log_a):
    b = jnp.cumsum(log_a, axis=1)
    w = jnp.exp(b[:, -1:] - b)
    return jnp.einsum('bthk,bthv->bhkv', k * w, v)


def gla_norm_gate(o, z_og, g_gla):
    B, T = o.shape[:2]
    o = o * lax.rsqrt(jnp.mean(o * o, axis=-1, keepdims=True) + EPS) * g_gla.astype(jnp.float32)
    return o.reshape(B, T, GLA_V) * jax.nn.silu(z_og.astype(jnp.float32))


def fourier_mix(z_f):
    B, T, _ = z_f.shape
    fg = z_f.astype(jnp.float32).reshape(B, T, FNET_GROUPS, FNET_GD)
    y = jnp.fft.fft2(fg, axes=(1, 3), norm='ortho').real
    return y.reshape(B, T, FNET_W)


def short_conv(u, w, b):
    L = u.shape[2]
    up = jnp.pad(u, ((0, 0), (0, 0), (CONV_LEFT, CONV_W - 1 - CONV_LEFT), (0, 0)))
    w = w.astype(jnp.float32)
    out = b.astype(jnp.float32)
    for j in range(CONV_W):
        out = out + up[:, :, j:j + L] * w[j]
    return out


def rglru_coeffs(u, w_a, b_a, w_x, b_x, lam):
    B, T, _ = u.shape
    f32 = jnp.float32
    uh = u.reshape(B, T, RG_HEADS, RG_HD)
    r = jax.nn.sigmoid(jnp.einsum('bthi,hij->bthj', uh, w_a.astype(f32)).reshape(B, T, RG_W) + b_a.astype(f32))
    i = jax.nn.sigmoid(jnp.einsum('bthi,hij->bthj', uh, w_x.astype(f32)).reshape(B, T, RG_W) + b_x.astype(f32))
    log_a = -RG_C * r * jax.nn.softplus(-lam.astype(f32))
    return jnp.exp(log_a), jnp.sqrt(-jnp.expm1(2.0 * log_a)) * (i * u)


def _lin_combine(left, right):
    a_l, b_l = left
    a_r, b_r = right
    return a_l * a_r, a_r * b_l + b_r


def linear_scan(a, bx, h0):
    bx = bx.at[:, 0].add(a[:, 0] * h0)
    return lax.associative_scan(_lin_combine, (a, bx), axis=1)[1]


def rglru_bidir(u, w_rg_a, b_rg_a, w_rg_x, b_rg_x, rg_lam, h0_f, h0_b):
    a_f, bx_f = rglru_coeffs(u, w_rg_a[0], b_rg_a[0], w_rg_x[0], b_rg_x[0], rg_lam[0])
    a_b, bx_b = rglru_coeffs(u, w_rg_a[1], b_rg_a[1], w_rg_x[1], b_rg_x[1], rg_lam[1])
    h_f = linear_scan(a_f, bx_f, h0_f)
    h_b_rev = linear_scan(rev(a_b), rev(bx_b), h0_b)
    return h_f, h_b_rev


def combine_groups(o_gla, z_og, g_gla, z_f, h_rg, z_rg):
    y_gla = gla_norm_gate(o_gla, z_og, g_gla)
    y_f = fourier_mix(z_f)
    y_rg = h_rg * jax.nn.gelu(z_rg.astype(jnp.float32))
    return jnp.concatenate([y_gla, y_f, y_rg], axis=-1).astype(z_og.dtype)


def context_mixer(zc, w_dec, b_dec, g_gla, w_conv, b_conv, w_rg_a, b_rg_a, w_rg_x, b_rg_x, rg_lam, with_output):
    z_q, z_k, z_v, z_dec, z_og, z_f, z_rx, z_rg = jnp.split(zc, IN_SPLITS, axis=-1)
    B = zc.shape[0]
    q, k, v, la_f, la_b = gla_prepare(z_q, z_k, z_v, z_dec, w_dec, b_dec)
    u = short_conv(z_rx.astype(jnp.float32)[:, None], w_conv, b_conv)[:, 0]
    zeros_h = jnp.zeros((B, RG_W), jnp.float32)
    h_f, h_b_rev = rglru_bidir(u, w_rg_a, b_rg_a, w_rg_x, b_rg_x, rg_lam, zeros_h, zeros_h)
    if not with_output:
        states = (gla_final_state(k, v, la_f), gla_final_state(rev(k), rev(v), rev(la_b)),
                  h_f[:, -1], h_b_rev[:, -1])
        return states, None
    s0 = jnp.zeros((B, GLA_HEADS, GLA_DK, GLA_DV), jnp.float32)
    o_f, s_f = gla_scan(q, k, v, la_f, s0)
    o_b_rev, s_b = gla_scan(rev(q), rev(k), rev(v), rev(la_b), s0)
    y = combine_groups(o_f + rev(o_b_rev), z_og, g_gla, z_f, h_f + rev(h_b_rev), z_rg)
    return (s_f, s_b, h_f[:, -1], h_b_rev[:, -1]), y


def latent_mixer(zl, states, w_dec, b_dec, g_gla, w_conv, b_conv, w_rg_a, b_rg_a, w_rg_x, b_rg_x, rg_lam):
    z_q, z_k, z_v, z_dec, z_og, z_f, z_rx, z_rg = jnp.split(zl, IN_SPLITS, axis=-1)
    B, T, _ = zl.shape
    rows = T // GRID_W
    s_f, s_b, hf0, hb0 = states
    q, k, v, la_f, la_b = gla_prepare(z_q, z_k, z_v, z_dec, w_dec, b_dec)
    o_f, _ = gla_scan(q, k, v, la_f, s_f)
    o_b_rev, _ = gla_scan(rev(q), rev(k), rev(v), rev(la_b), s_b)
    u = short_conv(z_rx.astype(jnp.float32).reshape(B, rows, GRID_W, RG_W), w_conv, b_conv).reshape(B, T, RG_W)
    h_f, h_b_rev = rglru_bidir(u, w_rg_a, b_rg_a, w_rg_x, b_rg_x, rg_lam, hf0, hb0)
    return combine_groups(o_f + rev(o_b_rev), z_og, g_gla, z_f, h_f + rev(h_b_rev), z_rg)


def setup_inputs(seed: int = 0) -> dict:
    key = jax.random.key(seed)
    ks = jax.random.split(key, 26)
    D = D_MODEL

    def nrm(k, shape, s):
        return jax.random.normal(k, shape, jnp.float32) * s

    u = jax.random.uniform(ks[19], (DEPTH, 2, RG_W), jnp.float32, 0.9, 0.999)
    p = u ** (1.0 / RG_C)
    rg_lam = jnp.log(p) - jnp.log1p(-p)
    return {
        'x': nrm(ks[0], (BATCH, SEQ, D), 1.0),
        'c': nrm(ks[1], (BATCH, D), 1.0),
        'ctx': nrm(ks[2], (BATCH, CTX_LEN, D), 1.0),
        'c_ctx': nrm(ks[3], (D,), 1.0),
        'w_ada': nrm(ks[4], (DEPTH, D, 6 * D), D ** -0.5),
        'b_ada': nrm(ks[5], (DEPTH, 6 * D), 0.01),
        'g_pre_mix': 1.0 + nrm(ks[6], (DEPTH, D), 0.05),
        'g_post_mix': 1.0 + nrm(ks[7], (DEPTH, D), 0.05),
        'g_pre_ffn': 1.0 + nrm(ks[8], (DEPTH, D), 0.05),
        'g_post_ffn': 1.0 + nrm(ks[9], (DEPTH, D), 0.05),
        'w_in': nrm(ks[10], (DEPTH, D, N_IN), D ** -0.5),
        'w_dec': nrm(ks[11], (DEPTH, 2, GLA_RANK, GLA_QK), GLA_RANK ** -0.5),
        'b_dec': nrm(ks[12], (DEPTH, 2, GLA_QK), 0.1),
        'g_gla': 1.0 + nrm(ks[13], (DEPTH, GLA_DV), 0.05),
        'w_conv': nrm(ks[14], (DEPTH, CONV_W, RG_W), CONV_W ** -0.5),
        'b_conv': nrm(ks[15], (DEPTH, RG_W), 0.01),
        'w_rg_a': nrm(ks[16], (DEPTH, 2, RG_HEADS, RG_HD, RG_HD), RG_HD ** -0.5),
        'b_rg_a': nrm(ks[17], (DEPTH, 2, RG_W), 0.01),
        'w_rg_x': nrm(ks[18], (DEPTH, 2, RG_HEADS, RG_HD, RG_HD), RG_HD ** -0.5),
        'b_rg_x': nrm(ks[20], (DEPTH, 2, RG_W), 0.01),
        'rg_lam': rg_lam,
        'w_out': nrm(ks[21], (DEPTH, MIX_W, D), MIX_W ** -0.5),
        'w_ffn_gate': nrm(ks[22], (DEPTH, D, D_FF), D ** -0.5),
        'w_ffn_up': nrm(ks[23], (DEPTH, D, D_FF), D ** -0.5),
        'w_ffn_down': nrm(ks[24], (DEPTH, D_FF, D), D_FF ** -0.5),
    }


def reference(x, c, ctx, c_ctx, w_ada, b_ada, g_pre_mix, g_post_mix, g_pre_ffn, g_post_ffn,
              w_in, w_dec, b_dec, g_gla, w_conv, b_conv, w_rg_a, b_rg_a, w_rg_x, b_rg_x,
              rg_lam, w_out, w_ffn_gate, w_ffn_up, w_ffn_down):
    silu_c = jax.nn.silu(c)
    silu_cc = jax.nn.silu(c_ctx)
    h_ctx = ctx
    for l in range(DEPTH):
        last = l == DEPTH - 1
        mod = (silu_c @ w_ada[l] + b_ada[l])[:, None, :]
        mod_c = silu_cc @ w_ada[l] + b_ada[l]
        sh1, sc1, gt1, sh2, sc2, gt2 = jnp.split(mod, 6, axis=-1)
        csh1, csc1, cgt1, csh2, csc2, cgt2 = jnp.split(mod_c, 6, axis=-1)

        hc = modulate(rmsnorm(h_ctx, g_pre_mix[l]), csh1, csc1)
        states, yc = context_mixer(hc @ w_in[l], w_dec[l], b_dec[l], g_gla[l], w_conv[l], b_conv[l],
                                   w_rg_a[l], b_rg_a[l], w_rg_x[l], b_rg_x[l], rg_lam[l],
                                   with_output=not last)
        hl = modulate(rmsnorm(x, g_pre_mix[l]), sh1, sc1)
        yl = latent_mixer(hl @ w_in[l], states, w_dec[l], b_dec[l], g_gla[l], w_conv[l], b_conv[l],
                          w_rg_a[l], b_rg_a[l], w_rg_x[l], b_rg_x[l], rg_lam[l])
        x = x + gt1 * rmsnorm(yl @ w_out[l], g_post_mix[l])

        hf = modulate(rmsnorm(x, g_pre_ffn[l]), sh2, sc2)
        x = x + gt2 * rmsnorm(swiglu(hf, w_ffn_gate[l], w_ffn_up[l], w_ffn_down[l]), g_post_ffn[l])

        if not last:
            h_ctx = h_ctx + cgt1 * rmsnorm(yc @ w_out[l], g_post_mix[l])
            hfc = modulate(rmsnorm(h_ctx, g_pre_ffn[l]), csh2, csc2)
            h_ctx = h_ctx + cgt2 * rmsnorm(swiglu(hfc, w_ffn_gate[l], w_ffn_up[l], w_ffn_down[l]), g_post_ffn[l])
    return x
```

```python
import contextlib
import numpy as np
import ml_dtypes
import concourse.bass as bass
import concourse.mybir as mybir
from concourse.bass_utils import run_bass_kernel_spmd

F32 = mybir.dt.float32
BF16 = mybir.dt.bfloat16
AF = mybir.ActivationFunctionType
ALU = mybir.AluOpType

D = 1024; TL = 4096; TC = 256; NT = TL + TC; L = 4
NIN = 2336; DFF = 2816; NCH = NT // 128
EPS = 1e-6
TT = [(0, 256)] + [(256 + 512 * i, 512) for i in range(8)]
TT256 = [(256 * i, 256) for i in range(17)]
FMCOLS = [0, 128, 256, 384, 1056, 1184, 1312, 1440, 1568, 1696, 1824, 1952, 2080, 2208]
ZQ, ZK, ZOG, ZF, ZRX, ZRG = 0, 2, 4, 8, 10, 12

_off = {}
_ns = 0
def _reg_small(name, n):
    global _ns
    _off[name] = _ns; _ns += n
_reg_small("b_ada", L * 48); _reg_small("gain", L * 32); _reg_small("g_gla", L)
_reg_small("w_conv", L * 8); _reg_small("b_conv", L * 2); _reg_small("b_rg_a", L * 4)
_reg_small("b_rg_x", L * 4); _reg_small("lam", L * 4); _reg_small("b_dec", L * 4)
NS = _ns


class Reg:
    __slots__ = ("w", "rl", "rd", "noread")
    def __init__(self):
        self.w = None; self.rl = {}; self.rd = []; self.noread = False

class Ins:
    __slots__ = ("eng", "fn", "deps", "need", "stream", "ticket", "is_dma")
    def __init__(self, eng, fn):
        self.eng = eng; self.fn = fn; self.deps = []; self.need = False
        self.stream = None; self.ticket = None; self.is_dma = False

ENGS = ("pe", "act", "dve", "pool", "sp")
NDMASEM = 8

class Prog:
    def __init__(self, nc, es):
        self.nc = nc
        self.q = {e: [] for e in ENGS}
        self.ndma = {e: 0 for e in ENGS}
        self.dma_last = {}
        self.cnt = {e: 0 for e in ENGS}
        self.seen = {e: {} for e in ENGS}
        self.last = {e: None for e in ENGS}
        self.regs = []
        self.bar = None
        self.sems = {}
        for e in ENGS:
            self.sems[e] = es.enter_context(nc.semaphore("s_" + e))
        for k in range(NDMASEM):
            self.sems[("sp", k)] = es.enter_context(nc.semaphore("d_sp%d" % k))

    def reg(self, persist=False):
        r = Reg()
        if persist: self.regs.append(r)
        return r

    def _track(self, ins, reads, writes, acc):
        deps = ins.deps
        for t in reads:
            if t.w is not None: deps.append(t.w)
        for t in writes:
            if t.w is not None: deps.append(t.w)
            deps.extend(t.rl.values()); deps.extend(t.rd)
        for t in reads:
            if t.noread: continue
            if ins.is_dma: t.rd.append(ins)
            else: t.rl[ins.eng] = ins
        for t in writes:
            t.w = ins; t.rl = {}; t.rd = []
        for t in acc:
            t.w = ins
        if self.bar is not None and self.bar[0].get(ins.eng):
            deps.extend(self.bar[1]); self.bar[0][ins.eng] = False
        for d in deps: d.need = True

    def op(self, eng, fn, reads=(), writes=(), acc=()):
        ins = Ins(eng, fn)
        self._track(ins, reads, writes, acc)
        self.q[eng].append(ins)
        return ins

    def dma(self, out, in_, reads=(), writes=(), eng="sp", **kw):
        ins = Ins(eng, lambda e: e.dma_start(out=out, in_=in_, **kw))
        ins.is_dma = True
        i = self.ndma[eng]; self.ndma[eng] += 1
        ins.stream = (eng, i % NDMASEM); ins.ticket = 16 * (i // NDMASEM + 1)
        prev = self.dma_last.get(ins.stream)
        if prev is not None: ins.deps.append(prev)
        self.dma_last[ins.stream] = ins
        ins.need = True
        self._track(ins, reads, writes, ())
        self.q[eng].append(ins)
        return ins

    def barrier(self):
        pend = [x for x in self.last.values() if x is not None] + list(self.dma_last.values())
        for e in ENGS:
            if self.q[e]: pend.append(self.q[e][-1])
        for x in pend: x.need = True
        self.bar = ({e: True for e in ENGS}, pend)

    def flush(self, final=False):
        nc = self.nc
        for t in self.regs:
            if t.w is not None: t.w.need = True
            for r in t.rl.values(): r.need = True
            for r in t.rd: r.need = True
        for e in ENGS:
            if self.q[e]: self.q[e][-1].need = True
        for e in ENGS:
            for ins in self.q[e]:
                if ins.is_dma or ins.ticket is not None: continue
                ins.stream = e
                if ins.need:
                    self.cnt[e] += 1; ins.ticket = self.cnt[e]
        sems = self.sems
        def run(e, eng):
            seen = self.seen[e]
            for ins in self.q[e]:
                need = {}
                for d in ins.deps:
                    if d.ticket is None:
                        raise RuntimeError("dep without ticket")
                    if d.ticket > need.get(d.stream, 0): need[d.stream] = d.ticket
                for s, v in need.items():
                    if seen.get(s, 0) >= v: continue
                    eng.wait_ge(sems[s], v); seen[s] = v
                r = ins.fn(eng)
                if ins.need:
                    r.then_inc(sems[ins.stream], 16 if ins.is_dma else 1)
            if final:
                for s, lastd in self.dma_last.items():
                    if s[0] == e and seen.get(s, 0) < lastd.ticket:
                        eng.wait_ge(sems[s], lastd.ticket); seen[s] = lastd.ticket
        with nc.Block() as block:
            if self.q["sp"]:
                @block.sync
                def _(eng): run("sp", eng)
            if self.q["pe"]:
                @block.tensor
                def _(eng): run("pe", eng)
            if self.q["act"]:
                @block.scalar
                def _(eng): run("act", eng)
            if self.q["dve"]:
                @block.vector
                def _(eng): run("dve", eng)
            if self.q["pool"]:
                @block.gpsimd
                def _(eng): run("pool", eng)
        for e in ENGS:
            if self.q[e]: self.last[e] = self.q[e][-1]
            self.q[e] = []


class Tile:
    def __init__(self, t, r):
        self.t = t; self.r = r


def build(n_layers=L, dump=(), stop=None):
    nc = bass.Bass("TRN2", target_bir_lowering=False)
    top = contextlib.ExitStack()
    P = Prog(nc, top)

    def din(name, shape, dt=F32):
        return nc.dram_tensor(name, shape, dt, kind="ExternalInput").ap()
    def dscr(name, shape, dt):
        kind = "ExternalOutput" if name in dump else "Internal"
        return nc.dram_tensor(name, shape, dt, kind=kind).ap()

    xT_d = din("xT", [D, NT]); cc_d = din("cc", [128, 16]); sm_d = din("smalls", [128, NS])
    w_ada_d = din("w_ada", [L, D, 6 * D]); w_in_d = din("w_in", [L, D, NIN]); w_out_d = din("w_out", [L, D, D])
    w_g_d = din("w_ffn_gate", [L, D, DFF]); w_u_d = din("w_ffn_up", [L, D, DFF]); w_d_d = din("w_ffn_down", [L, DFF, D])
    w_dec_d = din("w_dec", [L, 2, 16, 256]); wrg_d = din("wrg", [L, 2, 2, 2, 128, 128])
    NCB = 128 * 2 + 256 + 128 + 4 * 256
    cb_d = din("cbf", [128, NCB], BF16)
    m3_d = din("m3", [128, 4096], BF16)
    cf_d = din("cf32", [128, 256])
    out_d = nc.dram_tensor("outT", [D, TL], F32, kind="ExternalOutput").ap()

    res_d = dscr("res", [D, NT], F32)
    zT_d = dscr("zT", [14 * 128, NT], BF16)
    dec_d = dscr("decT", [32, NT], F32)
    vtok_d = dscr("vtok", [NT, 512], BF16)
    yT_d = dscr("yT", [D, NT], BF16)
    Yd_d = dscr("Yd", [TL, 512], BF16)
    Ad_d = dscr("Ad", [128, 64 * 256], BF16)
    R_res, R_z, R_dec, R_v, R_y, R_Yd, R_Ad = (P.reg(True) for _ in range(7))

    uid = {"i": 0}
    def SB(es, name, shape, dt=F32):
        uid["i"] += 1
        return es.enter_context(nc.sbuf_tensor("%s_%d" % (name, uid["i"]), shape, dt))
    def mk(es, name, shape, dt=F32, persist=False):
        return Tile(SB(es, name, shape, dt), P.reg(persist))

    psb = [Tile(top.enter_context(nc.psum_tensor("ps%d" % i, [128, 512], F32)), P.reg(True)) for i in range(7)]
    psT = Tile(top.enter_context(nc.psum_tensor("psT", [128, 1024], BF16)), P.reg(True))
    pstate = {"i": 0}
    def psum():
        p = psb[pstate["i"] % 7]; pstate["i"] += 1; return p

    cb = mk(top, "cb", [128, NCB], BF16, True)
    cf = mk(top, "cf", [128, 256], F32, True)
    sm = mk(top, "sm", [128, NS], F32, True)
    P.dma(cb.t[:], cb_d, writes=[cb.r]); P.dma(cf.t[:], cf_d, writes=[cf.r]); P.dma(sm.t[:], sm_d, writes=[sm.r])
    ones_b = cb.t[:, 0:128]; ident_b = cb.t[:, 128:256]; CD = cb.t[:, 256:512]; W1 = cb.t[:, 512:640]
    C256 = cb.t[:, 640:640 + 512].rearrange("p (a b) -> p a b", a=2)
    S256 = cb.t[:, 1152:1152 + 512].rearrange("p (a b) -> p a b", a=2)
    maskFB = cf.t[:, 0:256]
    def smv(name, idx, n=1):
        o = _off[name] + idx
        return sm.t[:, o:o + n]

    mod = mk(top, "mod", [128, L * 96], F32, True)
    A1 = mk(top, "A1", [128, L * 16], F32, True); G1 = mk(top, "G1", [128, L * 16], F32, True)
    A2 = mk(top, "A2", [128, L * 16], F32, True); G2 = mk(top, "G2", [128, L * 16], F32, True)
    cl = mk(top, "cl", [128, L * 4], F32, True); cl2 = mk(top, "cl2", [128, L * 4], F32, True); nbdec = mk(top, "nbdec", [128, L * 4], F32, True)
    def modv(l, g, fc, j):
        o = l * 96 + g * 16 + fc * 2 + j
        return mod.t[:, o:o + 1]
    def pv(tile_, l, fc, j):
        o = l * 16 + fc * 2 + j
        return tile_.t[:, o:o + 1]

    def act(out, in_, func, reads, writes, scale=1.0, bias=0.0, eng="act"):
        return P.op("act", lambda e: e.activation(out=out, in_=in_, func=func, scale=scale, bias=bias), reads=reads, writes=writes)
    def tt(eng, out, a, b, op, reads, writes):
        return P.op(eng, lambda e: e.tensor_tensor(out=out, in0=a, in1=b, op=op), reads=reads, writes=writes)
    def ts(eng, out, a, s1, s2, op0, op1, reads, writes):
        if s2 is None:
            return P.op(eng, lambda e: e.tensor_scalar(out=out, in0=a, scalar1=s1, scalar2=None, op0=op0), reads=reads, writes=writes)
        return P.op(eng, lambda e: e.tensor_scalar(out=out, in0=a, scalar1=s1, scalar2=s2, op0=op0, op1=op1), reads=reads, writes=writes)
    def stt(eng, out, a, s, b, op0, op1, reads, writes):
        return P.op(eng, lambda e: e.scalar_tensor_tensor(out=out, in0=a, scalar=s, in1=b, op0=op0, op1=op1), reads=reads, writes=writes)
    def cp(eng, out, in_, reads, writes):
        if eng == "act":
            return P.op("act", lambda e: e.activation(out=out, in_=in_, func=AF.Copy), reads=reads, writes=writes)
        return P.op(eng, lambda e: e.tensor_copy(out=out, in_=in_), reads=reads, writes=writes)
    def mmx(ps, out, lhsT, rhs, start, stop, reads, newtile):
        if newtile:
            return P.op("pe", lambda e: e.matmul(out, lhsT=lhsT, rhs=rhs, start=start, stop=stop), reads=reads, writes=[ps.r])
        return P.op("pe", lambda e: e.matmul(out, lhsT=lhsT, rhs=rhs, start=start, stop=stop), reads=reads, acc=[ps.r])
    def mm(ps, out, lhsT, rhs, first, last, reads):
        return mmx(ps, out, lhsT, rhs, first, last, reads, first)

    evs = {"i": 0}
    def evac(out, in_, reads, writes):
        evs["i"] += 1
        return cp("act" if evs["i"] % 2 else "dve", out, in_, reads, writes)

    def rstd_from_ps(ps, N, lnv, rstd, inv_n):
        act(lnv.t[:, :N], ps.t[:, :N], AF.Ln, [ps.r], [lnv.r], scale=inv_n, bias=EPS)
        act(rstd.t[:, :N], lnv.t[:, :N], AF.Exp, [lnv.r], [rstd.r], scale=-0.5)

    class WLoader:
        def __init__(self, es, ncols=1024):
            self.st = [mk(es, "wst%d" % i, [128, ncols]) for i in range(2)]
            self.i = 0; self.nc = ncols
        def load(self, dst, dreg, src, ncols):
            for c0 in range(0, ncols, self.nc):
                n = min(self.nc, ncols - c0)
                st = self.st[self.i % 2]; self.i += 1
                P.dma(st.t[:, :n], src[:, c0:c0 + n], writes=[st.r])
                cp("pool", dst[:, c0:c0 + n], st.t[:, :n], [st.r], [dreg])

    with contextlib.ExitStack() as ph:
        cc = mk(ph, "cc_sb", [128, 16]); sv = mk(ph, "sv", [128, 16])
        P.dma(cc.t[:], cc_d, writes=[cc.r])
        act(sv.t[:], cc.t[:], AF.Silu, [cc.r], [sv.r])
        wst = [mk(ph, "wada%d" % i, [128, 8, 1024]) for i in range(2)]
        k = 0
        for l in range(n_layers):
            ps = psum()
            for g in range(6):
                w = wst[k % 2]; k += 1
                P.dma(w.t[:], w_ada_d[l, :, g * 1024:(g + 1) * 1024].rearrange("(kc p) n -> p kc n", p=128), writes=[w.r])
                for fc in range(8):
                    o = (g * 8 + fc) * 2
                    for kc in range(8):
                        mmx(ps, ps.t[:, o:o + 2], w.t[:, kc, fc * 128:(fc + 1) * 128], sv.t[:, kc * 2:kc * 2 + 2],
                            kc == 0, kc == 7, [w.r, sv.r], (g == 0 and fc == 0 and kc == 0))
            mv = mod.t[:, l * 96:(l + 1) * 96].rearrange("p (a j) -> p a j", j=2)
            pv2 = ps.t[:, 0:96].rearrange("p (a j) -> p a j", j=2)
            for j in range(2):
                tt("dve", mv[:, :, j], pv2[:, :, j], smv("b_ada", l * 48, 48), ALU.add, [ps.r, sm.r], [mod.r])
            for (dst, gsc, gidx) in ((A1, 1, 0), (A2, 4, 2)):
                dv = dst.t[:, l * 16:(l + 1) * 16].rearrange("p (a j) -> p a j", j=2)
                scv = mod.t[:, l * 96 + gsc * 16: l * 96 + gsc * 16 + 16].rearrange("p (a j) -> p a j", j=2)
                for j in range(2):
                    stt("dve", dv[:, :, j], scv[:, :, j], 1.0, smv("gain", l * 32 + gidx * 8, 8), ALU.add, ALU.mult, [mod.r, sm.r], [dst.r])
            for (dst, ggt, gidx) in ((G1, 2, 1), (G2, 5, 3)):
                dv = dst.t[:, l * 16:(l + 1) * 16].rearrange("p (a j) -> p a j", j=2)
                gv = mod.t[:, l * 96 + ggt * 16: l * 96 + ggt * 16 + 16].rearrange("p (a j) -> p a j", j=2)
                for j in range(2):
                    tt("dve", dv[:, :, j], gv[:, :, j], smv("gain", l * 32 + gidx * 8, 8), ALU.mult, [mod.r, sm.r], [dst.r])
        tmp = mk(ph, "tmpl", [128, L * 4])
        act(tmp.t[:], smv("lam", 0, L * 4), AF.Exp, [sm.r], [tmp.r], scale=-1.0)
        act(tmp.t[:], tmp.t[:], AF.Ln, [tmp.r], [tmp.r], scale=1.0, bias=1.0)
        ts("dve", cl.t[:], tmp.t[:], -8.0, None, ALU.mult, None, [tmp.r], [cl.r])
        ts("dve", cl2.t[:], tmp.t[:], -16.0, None, ALU.mult, None, [tmp.r], [cl2.r])
        ts("dve", nbdec.t[:], smv("b_dec", 0, L * 4), -1.0, None, ALU.mult, None, [sm.r], [nbdec.r])
        P.flush()
    P.barrier()
    for t_ in (cb, cf, sm, mod, A1, G1, A2, G2, cl, cl2, nbdec):
        t_.r.noread = True

    def finish():
        P.flush(final=True)
        top.close()
        return nc

    for l in range(n_layers):
        src_d = xT_d if l == 0 else res_d
        src_reads = [] if l == 0 else [R_res]
        with contextlib.ExitStack() as ph:
            Win = mk(ph, "Win", [128, 8, NIN], BF16)
            wl = WLoader(ph, 1168)
            for kc in range(8):
                wl.load(Win.t[:, kc, :], Win.r, w_in_d[l, kc * 128:(kc + 1) * 128, :], NIN)
            xs = [mk(ph, "xs%d" % i, [128, 8, 512]) for i in range(2)]
            sq = mk(ph, "sq", [128, 8, 512], BF16)
            h = mk(ph, "h", [128, 8, 512], BF16)
            lnv = mk(ph, "lnv", [128, 512]); rstd = mk(ph, "rstd", [128, 512])
            tmp = [mk(ph, "tmp%d" % i, [128, 512]) for i in range(2)]
            zst = [mk(ph, "zst%d" % i, [128, 14, 512], BF16) for i in range(2)]
            vst = mk(ph, "vst", [128, 4, 512], BF16)
            dst_ = mk(ph, "dst", [32, 512])
            def load_x(ti):
                t0, N = TT[ti]
                x_ = xs[ti % 2]
                P.dma(x_.t[:, :, :N], src_d[:, t0:t0 + N].rearrange("(c p) t -> p c t", p=128), reads=src_reads, writes=[x_.r])
            load_x(0)
            for ti, (t0, N) in enumerate(TT):
                j = 1 if ti == 0 else 0
                if ti + 1 < len(TT): load_x(ti + 1)
                x_ = xs[ti % 2]
                tt("pool", sq.t[:, :, :N], x_.t[:, :, :N], x_.t[:, :, :N], ALU.mult, [x_.r], [sq.r])
                ps = psum()
                for c in range(8):
                    mm(ps, ps.t[:, :N], ones_b, sq.t[:, c, :N], c == 0, c == 7, [cb.r, sq.r])
                rstd_from_ps(ps, N, lnv, rstd, 1.0 / D)
                for c in range(8):
                    tm = tmp[c % 2]
                    tt("dve", tm.t[:, :N], x_.t[:, c, :N], rstd.t[:, :N], ALU.mult, [x_.r, rstd.r], [tm.r])
                    act(h.t[:, c, :N], tm.t[:, :N], AF.Identity, [tm.r, A1.r, mod.r], [h.r], scale=pv(A1, l, c, j), bias=modv(l, 0, c, j))
                zs = zst[ti % 2]
                for ci, col in enumerate(FMCOLS):
                    ps = psum()
                    for kc in range(8):
                        mm(ps, ps.t[:, :N], Win.t[:, kc, col:col + 128], h.t[:, kc, :N], kc == 0, kc == 7, [Win.r, h.r])
                    evac(zs.t[:, ci, :N], ps.t[:, :N], [ps.r], [zs.r])
                P.dma(zT_d[:, t0:t0 + N].rearrange("(c p) t -> p c t", p=128), zs.t[:, :, :N], reads=[zs.r], writes=[R_z])
                ps = psum()
                for kc in range(8):
                    mm(ps, ps.t[0:32, :N], Win.t[:, kc, 1024:1056], h.t[:, kc, :N], kc == 0, kc == 7, [Win.r, h.r])
                evac(dst_.t[:, :N], ps.t[0:32, :N], [ps.r], [dst_.r])
                P.dma(dec_d[:, t0:t0 + N], dst_.t[:, :N], reads=[dst_.r], writes=[R_dec])
                ns = N // 128
                for s in range(ns):
                    ps = psum()
                    for kc in range(8):
                        mm(ps, ps.t[:, :512], h.t[:, kc, s * 128:(s + 1) * 128], Win.t[:, kc, 512:1024], kc == 0, kc == 7, [Win.r, h.r])
                    evac(vst.t[:, s, :], ps.t[:, :512], [ps.r], [vst.r])
                P.dma(vtok_d[t0:t0 + N, :].rearrange("(s p) v -> p s v", p=128), vst.t[:, :ns, :], reads=[vst.r], writes=[R_v])
            P.flush()
        P.barrier()
        if stop == ("A", l): return finish()

        with contextlib.ExitStack() as ph:
            S = [mk(ph, "S%d" % i, [128, NT]) for i in range(6)]
            H = [mk(ph, "H%d" % i, [128, NT], BF16) for i in range(3)]
            wst = mk(ph, "wrgst", [128, 4, 128]); wb = mk(ph, "wrgb", [128, 4, 128], BF16)
            for ch in range(2):
                P.dma(H[0].t[:], zT_d[(ZRX + ch) * 128:(ZRX + ch + 1) * 128, :], reads=[R_z], writes=[H[0].r])
                P.dma(wst.t[:], wrg_d[l, :, :, ch].rearrange("d g k m -> k (d g) m"), writes=[wst.r])
                cp("pool", wb.t[:], wst.t[:], [wst.r], [wb.r])
                z = H[0]; u = S[0]
                wc = lambda tap: smv("w_conv", l * 8 + ch * 4 + tap)
                ts("dve", u.t[:], z.t[:], wc(1), smv("b_conv", l * 2 + ch), ALU.mult, ALU.add, [z.r, sm.r], [u.r])
                for (lo, n, rows, w) in ((0, TC, 1, TC), (TC, TL, 64, 64)):
                    zv = z.t[:, lo:lo + n].rearrange("p (r w) -> p r w", w=w)
                    uv = u.t[:, lo:lo + n].rearrange("p (r w) -> p r w", w=w)
                    stt("dve", uv[:, :, 1:w], zv[:, :, 0:w - 1], wc(0), uv[:, :, 1:w], ALU.mult, ALU.add, [z.r, sm.r, u.r], [u.r])
                    stt("dve", uv[:, :, 0:w - 1], zv[:, :, 1:w], wc(2), uv[:, :, 0:w - 1], ALU.mult, ALU.add, [z.r, sm.r, u.r], [u.r])
                    stt("dve", uv[:, :, 0:w - 2], zv[:, :, 2:w], wc(3), uv[:, :, 0:w - 2], ALU.mult, ALU.add, [z.r, sm.r, u.r], [u.r])
                ub = H[1]
                cp("act", ub.t[:], u.t[:], [u.r], [ub.r])
                for d in range(2):
                    r_, i_, a_ = S[1], S[2], S[3]
                    hd = S[4 + d]
                    for (gate, dstt, bname) in ((0, r_, "b_rg_a"), (1, i_, "b_rg_x")):
                        for (t0, N) in TT:
                            ps = psum()
                            mm(ps, ps.t[:, :N], wb.t[:, d * 2 + gate, :], ub.t[:, t0:t0 + N], True, True, [wb.r, ub.r])
                            act(dstt.t[:, t0:t0 + N], ps.t[:, :N], AF.Sigmoid, [ps.r, sm.r], [dstt.r], bias=smv(bname, l * 4 + d * 2 + ch))
                    o4 = l * 4 + d * 2 + ch
                    act(a_.t[:], r_.t[:], AF.Exp, [r_.r, cl.r], [a_.r], scale=cl.t[:, o4:o4 + 1])
                    act(r_.t[:], r_.t[:], AF.Exp, [r_.r, cl2.r], [r_.r], scale=cl2.t[:, o4:o4 + 1])
                    ts("dve", r_.t[:], r_.t[:], -1.0, 1.0, ALU.mult, ALU.add, [r_.r], [r_.r])
                    act(r_.t[:], r_.t[:], AF.Sqrt, [r_.r], [r_.r])
                    tt("pool", i_.t[:], i_.t[:], u.t[:], ALU.mult, [i_.r, u.r], [i_.r])
                    tt("dve", r_.t[:], r_.t[:], i_.t[:], ALU.mult, [r_.r, i_.r], [r_.r])
                    if d == 0:
                        P.op("dve", lambda e, hd=hd, a_=a_, r_=r_: e.tensor_tensor_scan(out=hd.t[:, 0:TC], data0=a_.t[:, 0:TC], data1=r_.t[:, 0:TC], initial=0.0, op0=ALU.mult, op1=ALU.add),
                             reads=[a_.r, r_.r], writes=[hd.r])
                        P.op("dve", lambda e, hd=hd, a_=a_, r_=r_: e.tensor_tensor_scan(out=hd.t[:, TC:NT], data0=a_.t[:, TC:NT], data1=r_.t[:, TC:NT], initial=hd.t[:, TC - 1:TC], op0=ALU.mult, op1=ALU.add),
                             reads=[a_.r, r_.r, hd.r], writes=[hd.r])
                    else:
                        P.op("dve", lambda e, hd=hd, a_=a_, r_=r_: e.tensor_tensor_scan(out=hd.t[:, 0:TC][:, ::-1], data0=a_.t[:, 0:TC][:, ::-1], data1=r_.t[:, 0:TC][:, ::-1], initial=0.0, op0=ALU.mult, op1=ALU.add),
                             reads=[a_.r, r_.r], writes=[hd.r])
                        P.op("dve", lambda e, hd=hd, a_=a_, r_=r_: e.tensor_tensor_scan(out=hd.t[:, TC:NT][:, ::-1], data0=a_.t[:, TC:NT][:, ::-1], data1=r_.t[:, TC:NT][:, ::-1], initial=hd.t[:, 0:1], op0=ALU.mult, op1=ALU.add),
                             reads=[a_.r, r_.r, hd.r], writes=[hd.r])
                tt("pool", S[4].t[:], S[4].t[:], S[5].t[:], ALU.add, [S[4].r, S[5].r], [S[4].r])
                P.dma(H[0].t[:], zT_d[(ZRG + ch) * 128:(ZRG + ch + 1) * 128, :], reads=[R_z], writes=[H[0].r])
                act(S[5].t[:], H[0].t[:], AF.Gelu, [H[0].r], [S[5].r])
                tt("dve", H[2].t[:], S[4].t[:], S[5].t[:], ALU.mult, [S[4].r, S[5].r], [H[2].r])
                P.dma(yT_d[768 + ch * 128: 768 + (ch + 1) * 128, :], H[2].t[:], reads=[H[2].r], writes=[R_y])
            P.flush()
        P.barrier()
        if stop == ("B1", l): return finish()

        with contextlib.ExitStack() as ph:
            mF = mk(ph, "mF", [128, NT + 1], BF16)
            P.op("pool", lambda e: e.memset(mF.t[:], 1.0), writes=[mF.r])
            P.op("pool", lambda e: e.memset(mF.t[:, 0:NT + 1:128], 0.0), writes=[mF.r])
            lrt = mk(ph, "lr", [48, NT])
            wdt = mk(ph, "wdec", [48, 256])
            for d in range(2):
                P.dma(lrt.t[d * 32:d * 32 + 16, :], dec_d[d * 16:(d + 1) * 16, :], reads=[R_dec], writes=[lrt.r])
                P.dma(wdt.t[d * 32:d * 32 + 16, :], w_dec_d[l, d], writes=[wdt.r])
            vtok = mk(ph, "vtok", [128, NCH, 256], BF16)
            Sa = mk(ph, "Sa", [128, NT])
            Hq = mk(ph, "Hq", [128, NT], BF16); Hk = mk(ph, "Hk", [128, NT], BF16)
            qt = [mk(ph, "qt%d" % d, [128, NT], BF16) for d in range(2)]
            kt = [mk(ph, "kt%d" % d, [128, NT], BF16) for d in range(2)]
            ktok = [mk(ph, "ktok%d" % d, [128, NCH, 128], BF16) for d in range(2)]
            ebc = [mk(ph, "ebc%d" % d, [128, NCH]) for d in range(2)]
            Sbf = [mk(ph, "Sbf%d" % d, [128, NCH, 256], BF16) for d in range(2)]
            Scur = [[mk(ph, "Sc%d%d" % (d, i), [128, 256]) for i in range(2)] for d in range(2)]
            kvs = [mk(ph, "kvs%d" % i, [128, 256]) for i in range(4)]
            am = [mk(ph, "am%d" % i, [128, 256], BF16) for i in range(3)]
            Hog = Hq; yg = Hq
            sqo = mk(ph, "sqo", [128, 512], BF16); lnv = mk(ph, "lnv2", [128, 512]); rstd = mk(ph, "rstd2", [128, 512])
            t1 = mk(ph, "t1", [128, 512]); t2 = mk(ph, "t2", [128, 512])
            order = [list(range(NCH)), [1, 0] + list(range(NCH - 1, 1, -1))]
            for hp in range(2):
                P.dma(Hq.t[:], zT_d[(ZQ + hp) * 128:(ZQ + hp + 1) * 128, :], reads=[R_z], writes=[Hq.r])
                P.dma(Hk.t[:], zT_d[(ZK + hp) * 128:(ZK + hp + 1) * 128, :], reads=[R_z], writes=[Hk.r])
                for q4 in range(2):
                    P.dma(vtok.t[:, q4 * 17:(q4 + 1) * 17, :], vtok_d[q4 * 17 * 128:(q4 + 1) * 17 * 128, hp * 256:(hp + 1) * 256].rearrange("(s p) v -> p s v", p=128), reads=[R_v], writes=[vtok.r])
                for d in range(2):
                    o4 = l * 4 + d * 2 + hp
                    for (t0, N) in TT:
                        ps = psum()
                        mm(ps, ps.t[:, :N], wdt.t[d * 32:d * 32 + 16, hp * 128:(hp + 1) * 128], lrt.t[d * 32:d * 32 + 16, t0:t0 + N], True, True, [wdt.r, lrt.r])
                        act(Sa.t[:, t0:t0 + N], ps.t[:, :N], AF.Exp, [ps.r, nbdec.r], [Sa.r], scale=-1.0, bias=nbdec.t[:, o4:o4 + 1])
                    act(Sa.t[:], Sa.t[:], AF.Ln, [Sa.r], [Sa.r], scale=1.0, bias=1.0)
                    if d == 0:
                        P.op("dve", lambda e: e.tensor_tensor_scan(out=Sa.t[:], data0=mF.t[:, 0:NT], data1=Sa.t[:], initial=0.0, op0=ALU.mult, op1=ALU.add),
                             reads=[mF.r, Sa.r], writes=[Sa.r])
                    else:
                        P.op("dve", lambda e: e.tensor_tensor_scan(out=Sa.t[:, ::-1], data0=mF.t[:, 1:NT + 1][:, ::-1], data1=Sa.t[:, ::-1], initial=0.0, op0=ALU.mult, op1=ALU.add),
                             reads=[mF.r, Sa.r], writes=[Sa.r])
                    act(Sa.t[:], Sa.t[:], AF.Exp, [Sa.r], [Sa.r], scale=-1.0 / 16.0)
                    stt("dve", qt[d].t[:], Hq.t[:], 0.125, Sa.t[:], ALU.mult, ALU.mult, [Hq.r, Sa.r], [qt[d].r])
                    e_at = 127 if d == 0 else 0
                    cp("pool", ebc[d].t[:], Sa.t[:, e_at:NT:128], [Sa.r], [ebc[d].r])
                    P.op("dve", lambda e: e.reciprocal(out=Sa.t[:], in_=Sa.t[:]), reads=[Sa.r, ebc[d].r], writes=[Sa.r])
                    tt("dve", kt[d].t[:], Hk.t[:], Sa.t[:], ALU.mult, [Hk.r, Sa.r], [kt[d].r])
                    for n0 in range(0, NCH, 4):
                        nn = min(4, NCH - n0)
                        for i in range(nn):
                            n = n0 + i
                            P.op("pe", lambda e, d=d, n=n, i=i: e.transpose(psT.t[:, i * 128:(i + 1) * 128], kt[d].t[:, n * 128:(n + 1) * 128], ident_b),
                                 reads=[kt[d].r, cb.r], **({"writes": [psT.r]} if i == 0 else {"acc": [psT.r]}))
                        evac(ktok[d].t[:, n0:n0 + nn, :], psT.t[:, :nn * 128].rearrange("p (a b) -> p a b", b=128), [psT.r], [ktok[d].r])
                cur = [0, 0]
                for d in range(2):
                    P.op("pool", lambda e, d=d: e.memset(Scur[d][0].t[:], 0.0), writes=[Scur[d][0].r])
                    P.op("pool", lambda e, d=d: e.memset(Sbf[d].t[:, order[d][0], :], 0.0), writes=[Sbf[d].r])
                kvi = 0
                for s in range(NCH - 1):
                    for d in range(2):
                        n = order[d][s]; nxt = order[d][s + 1]
                        ps = psum()
                        mm(ps, ps.t[:, :256], ktok[d].t[:, n, :], vtok.t[:, n, :], True, True, [ktok[d].r, vtok.r])
                        kv = kvs[kvi % 4]; kvi += 1
                        act(kv.t[:], ps.t[:, :256], AF.Identity, [ps.r, ebc[d].r], [kv.r], scale=ebc[d].t[:, n:n + 1])
                        sc_, sn_ = Scur[d][cur[d] % 2], Scur[d][(cur[d] + 1) % 2]; cur[d] += 1
                        stt("dve", sn_.t[:], sc_.t[:], ebc[d].t[:, n:n + 1], kv.t[:], ALU.mult, ALU.add, [sc_.r, ebc[d].r, kv.r], [sn_.r])
                        cp("pool", Sbf[d].t[:, nxt, :], sn_.t[:], [sn_.r], [Sbf[d].r])
                for hh in range(2):
                    head = 2 * hp + hh
                    Rw = slice(hh * 64, (hh + 1) * 64)
                    P.dma(Hog.t[:], zT_d[(ZOG + head) * 128:(ZOG + head + 1) * 128, :], reads=[R_z], writes=[Hog.r])
                    ai = 0
                    for (t0, N) in TT:
                        po = psum()
                        for ci in range(N // 128):
                            n = (t0 // 128) + ci
                            cs = slice(n * 128, (n + 1) * 128)
                            pa = psum()
                            for d in range(2):
                                mmx(pa, pa.t[:, d * 128:(d + 1) * 128], kt[d].t[Rw, cs], qt[d].t[Rw, cs], True, True, [kt[d].r, qt[d].r], d == 0)
                            a_ = am[ai % 3]; ai += 1
                            tt("dve", a_.t[:], pa.t[:, :256], maskFB, ALU.mult, [pa.r, cf.r], [a_.r])
                            oc = po.t[:, ci * 128:(ci + 1) * 128]
                            vv = vtok.t[:, n, hh * 128:(hh + 1) * 128]
                            first = (ci == 0)
                            mmx(po, oc, vv, a_.t[:, 0:128], True, False, [vtok.r, a_.r], first)
                            mmx(po, oc, vv, a_.t[:, 128:256], False, False, [vtok.r, a_.r], False)
                            mmx(po, oc, Sbf[0].t[Rw, n, hh * 128:(hh + 1) * 128], qt[0].t[Rw, cs], False, False, [Sbf[0].r, qt[0].r], False)
                            mmx(po, oc, Sbf[1].t[Rw, n, hh * 128:(hh + 1) * 128], qt[1].t[Rw, cs], False, True, [Sbf[1].r, qt[1].r], False)
                        act(sqo.t[:, :N], po.t[:, :N], AF.Square, [po.r], [sqo.r])
                        pss = psum()
                        mm(pss, pss.t[:, :N], ones_b, sqo.t[:, :N], True, True, [cb.r, sqo.r])
                        rstd_from_ps(pss, N, lnv, rstd, 1.0 / 128.0)
                        tt("dve", t1.t[:, :N], po.t[:, :N], rstd.t[:, :N], ALU.mult, [po.r, rstd.r], [t1.r])
                        act(t2.t[:, :N], Hog.t[:, t0:t0 + N], AF.Silu, [Hog.r], [t2.r])
                        stt("dve", yg.t[:, t0:t0 + N], t1.t[:, :N], smv("g_gla", l), t2.t[:, :N], ALU.mult, ALU.mult, [t1.r, t2.r, sm.r], [yg.r])
                    P.dma(yT_d[head * 128:(head + 1) * 128, :], yg.t[:], reads=[yg.r], writes=[R_y])
            P.flush()
        P.barrier()
        if stop == ("B2", l): return finish()

        with contextlib.ExitStack() as ph:
            fT = [mk(ph, "fT%d" % kc, [128, NT], BF16) for kc in range(2)]
            for kc in range(2):
                P.dma(fT[kc].t[:], zT_d[(ZF + kc) * 128:(ZF + kc + 1) * 128, :], reads=[R_z], writes=[fT[kc].r])
            Yc = mk(ph, "Yc", [128, 2, 512], BF16)
            Yl = mk(ph, "Yl", [128, 32, 512], BF16)
            for s in range(NCH):
                ps = psum()
                for kc in range(2):
                    mmx(ps, ps.t[:, kc * 256:(kc + 1) * 256], fT[kc].t[:, s * 128:(s + 1) * 128], CD, True, True, [fT[kc].r, cb.r], kc == 0)
                dst = Yc if s < 2 else Yl
                si = s if s < 2 else s - 2
                evac(dst.t[:, si, :].rearrange("p (ri kc c) -> p kc ri c", ri=2, kc=2),
                     ps.t[:, :512].rearrange("p (kc ri c) -> p kc ri c", kc=2, ri=2), [ps.r], [dst.r])
            for q4 in range(4):
                P.dma(Yd_d[q4 * 1024:(q4 + 1) * 1024, :].rearrange("(s p) v -> p s v", p=128), Yl.t[:, q4 * 8:(q4 + 1) * 8, :], reads=[Yl.r], writes=[R_Yd])
            yf = [mk(ph, "yf%d" % kc, [128, NT], BF16) for kc in range(2)]
            for kc in range(2):
                ps = psum()
                k4 = 0
                for tch in range(2):
                    for (ri, tab) in ((0, C256), (1, S256)):
                        mm(ps, ps.t[:, :256], Yc.t[:, tch, ri * 256 + kc * 128: ri * 256 + (kc + 1) * 128], tab[:, tch, :], k4 == 0, k4 == 3, [Yc.r, cb.r])
                        k4 += 1
                evac(yf[kc].t[:, 0:TC], ps.t[:, :256], [ps.r], [yf[kc].r])
            L1 = mk(ph, "L1", [128, 64, 256], BF16)
            for ri in range(2):
                P.dma(L1.t[ri * 64:(ri + 1) * 64, :, :],
                      Yd_d[:, ri * 256:(ri + 1) * 256].rearrange("(t1 t2) c -> t1 t2 c", t2=64), reads=[R_Yd], writes=[L1.r])
            A_sb = mk(ph, "A_sb", [128, 64 * 256], BF16)
            L1f = L1.t[:].rearrange("p a b -> p (a b)")
            for i in range(32):
                ps = psum()
                mm(ps, ps.t[:, :512], W1, L1f[:, i * 512:(i + 1) * 512], True, True, [cb.r, L1.r])
                evac(A_sb.t[:, i * 512:(i + 1) * 512], ps.t[:, :512], [ps.r], [A_sb.r])
            P.dma(Ad_d, A_sb.t[:], reads=[A_sb.r], writes=[R_Ad])
            L3 = mk(ph, "L3", [128, 64, 256], BF16)
            m3 = mk(ph, "m3", [128, 4096], BF16)
            P.dma(m3.t[:], m3_d, writes=[m3.r])
            M3 = m3.t[:].rearrange("p (a b) -> p a b", a=64)
            for ri in range(2):
                P.dma(L3.t[ri * 64:(ri + 1) * 64, :, :],
                      Ad_d[ri * 64:(ri + 1) * 64, :].rearrange("f1 (t2 c) -> t2 f1 c", c=256), reads=[R_Ad], writes=[L3.r])
            for kc in range(2):
                yv = yf[kc].t[:, TC:NT].rearrange("p (f2 f1) -> p f1 f2", f1=64)
                for g8 in range(8):
                    ps = psum()
                    for i in range(8):
                        f1 = g8 * 8 + i
                        mmx(ps, ps.t[:, i * 64:(i + 1) * 64], L3.t[:, f1, kc * 128:(kc + 1) * 128], M3[:, f1, :], True, True, [L3.r, m3.r], i == 0)
                    evac(yv[:, g8 * 8:(g8 + 1) * 8, :], ps.t[:, :512].rearrange("p (a b) -> p a b", b=64), [ps.r], [yf[kc].r])
                P.dma(yT_d[512 + kc * 128:512 + (kc + 1) * 128, :], yf[kc].t[:], reads=[yf[kc].r], writes=[R_y])
            P.flush()
        P.barrier()
        if stop == ("B3", l): return finish()

        with contextlib.ExitStack() as ph:
            Wo = mk(ph, "Wo", [128, 8, D], BF16)
            wl = WLoader(ph, 1024)
            for kc in range(8):
                wl.load(Wo.t[:, kc, :], Wo.r, w_out_d[l, kc * 128:(kc + 1) * 128, :], D)
            xs = [mk(ph, "xs%d" % i, [128, 8, 512]) for i in range(2)]
            ys = [mk(ph, "ys%d" % i, [128, 8, 512], BF16) for i in range(2)]
            mt = mk(ph, "mt", [128, 8, 512]); sq = mk(ph, "sq", [128, 8, 512], BF16)
            lnv = mk(ph, "lnv", [128, 512]); rstd = mk(ph, "rstd", [128, 512])
            tmp = [mk(ph, "tmp%d" % i, [128, 512]) for i in range(2)]
            def load_c(ti):
                t0, N = TT[ti]
                P.dma(xs[ti % 2].t[:, :, :N], src_d[:, t0:t0 + N].rearrange("(c p) t -> p c t", p=128), reads=src_reads, writes=[xs[ti % 2].r])
                P.dma(ys[ti % 2].t[:, :, :N], yT_d[:, t0:t0 + N].rearrange("(c p) t -> p c t", p=128), reads=[R_y], writes=[ys[ti % 2].r])
            load_c(0)
            for ti, (t0, N) in enumerate(TT):
                j = 1 if ti == 0 else 0
                if ti + 1 < len(TT): load_c(ti + 1)
                x_ = xs[ti % 2]; y_ = ys[ti % 2]
                for dc in range(8):
                    ps = psum()
                    for kc in range(8):
                        mm(ps, ps.t[:, :N], Wo.t[:, kc, dc * 128:(dc + 1) * 128], y_.t[:, kc, :N], kc == 0, kc == 7, [Wo.r, y_.r])
                    cp("act", mt.t[:, dc, :N], ps.t[:, :N], [ps.r], [mt.r])
                tt("pool", sq.t[:, :, :N], mt.t[:, :, :N], mt.t[:, :, :N], ALU.mult, [mt.r], [sq.r])
                ps = psum()
                for c in range(8):
                    mm(ps, ps.t[:, :N], ones_b, sq.t[:, c, :N], c == 0, c == 7, [cb.r, sq.r])
                rstd_from_ps(ps, N, lnv, rstd, 1.0 / D)
                for c in range(8):
                    tm = tmp[c % 2]
                    tt("dve", tm.t[:, :N], mt.t[:, c, :N], rstd.t[:, :N], ALU.mult, [mt.r, rstd.r], [tm.r])
                    stt("dve", x_.t[:, c, :N], tm.t[:, :N], pv(G1, l, c, j), x_.t[:, c, :N], ALU.mult, ALU.add, [tm.r, G1.r, x_.r], [x_.r])
                P.dma(res_d[:, t0:t0 + N].rearrange("(c p) t -> p c t", p=128), x_.t[:, :, :N], reads=[x_.r] + src_reads, writes=[R_res])
            P.flush()
        P.barrier()
        if stop == ("C", l): return finish()

        with contextlib.ExitStack() as ph:
            Wg = mk(ph, "Wg", [128, 8, DFF], BF16); Wu = mk(ph, "Wu", [128, 8, DFF], BF16); Wd = mk(ph, "Wd", [128, 22, D], BF16)
            wl = WLoader(ph, 512)
            for kc in range(8):
                wl.load(Wg.t[:, kc, :], Wg.r, w_g_d[l, kc * 128:(kc + 1) * 128, :], DFF)
                wl.load(Wu.t[:, kc, :], Wu.r, w_u_d[l, kc * 128:(kc + 1) * 128, :], DFF)
            for kc in range(22):
                wl.load(Wd.t[:, kc, :], Wd.r, w_d_d[l, kc * 128:(kc + 1) * 128, :], D)
            NB = 256
            xs = [mk(ph, "xs%d" % i, [128, 8, NB]) for i in range(2)]
            sq = mk(ph, "sq", [128, 8, NB], BF16); h = mk(ph, "h", [128, 8, NB], BF16)
            hid = mk(ph, "hid", [128, 22, NB], BF16); mt = mk(ph, "mt", [128, 8, NB])
            lnv = mk(ph, "lnv", [128, NB]); rstd = mk(ph, "rstd", [128, NB])
            tmp = [mk(ph, "tmp%d" % i, [128, NB]) for i in range(2)]
            sg = [mk(ph, "sg%d" % i, [128, NB]) for i in range(2)]
            last = (l == n_layers - 1)
            def load_d(ti):
                t0, N = TT256[ti]
                P.dma(xs[ti % 2].t[:], res_d[:, t0:t0 + N].rearrange("(c p) t -> p c t", p=128), reads=[R_res], writes=[xs[ti % 2].r])
            load_d(0)
            for ti, (t0, N) in enumerate(TT256):
                j = 1 if ti == 0 else 0
                if ti + 1 < len(TT256): load_d(ti + 1)
                x_ = xs[ti % 2]
                tt("pool", sq.t[:], x_.t[:], x_.t[:], ALU.mult, [x_.r], [sq.r])
                ps = psum()
                for c in range(8):
                    mm(ps, ps.t[:, :N], ones_b, sq.t[:, c, :], c == 0, c == 7, [cb.r, sq.r])
                rstd_from_ps(ps, N, lnv, rstd, 1.0 / D)
                for c in range(8):
                    tm = tmp[c % 2]
                    tt("dve", tm.t[:], x_.t[:, c, :], rstd.t[:], ALU.mult, [x_.r, rstd.r], [tm.r])
                    act(h.t[:, c, :], tm.t[:], AF.Identity, [tm.r, A2.r, mod.r], [h.r], scale=pv(A2, l, c, j), bias=modv(l, 3, c, j))
                for fc in range(22):
                    pg = psum(); pu = psum()
                    for kc in range(8):
                        mm(pg, pg.t[:, :N], Wg.t[:, kc, fc * 128:(fc + 1) * 128], h.t[:, kc, :], kc == 0, kc == 7, [Wg.r, h.r])
                    for kc in range(8):
                        mm(pu, pu.t[:, :N], Wu.t[:, kc, fc * 128:(fc + 1) * 128], h.t[:, kc, :], kc == 0, kc == 7, [Wu.r, h.r])
                    s_ = sg[fc % 2]
                    act(s_.t[:], pg.t[:, :N], AF.Silu, [pg.r], [s_.r])
                    tt("dve", hid.t[:, fc, :], s_.t[:], pu.t[:, :N], ALU.mult, [s_.r, pu.r], [hid.r])
                for dc in range(8):
                    ps = psum()
                    for kc in range(22):
                        mm(ps, ps.t[:, :N], Wd.t[:, kc, dc * 128:(dc + 1) * 128], hid.t[:, kc, :], kc == 0, kc == 21, [Wd.r, hid.r])
                    cp("act", mt.t[:, dc, :], ps.t[:, :N], [ps.r], [mt.r])
                tt("pool", sq.t[:], mt.t[:], mt.t[:], ALU.mult, [mt.r], [sq.r])
                ps = psum()
                for c in range(8):
                    mm(ps, ps.t[:, :N], ones_b, sq.t[:, c, :], c == 0, c == 7, [cb.r, sq.r])
                rstd_from_ps(ps, N, lnv, rstd, 1.0 / D)
                for c in range(8):
                    tm = tmp[c % 2]
                    tt("dve", tm.t[:], mt.t[:, c, :], rstd.t[:], ALU.mult, [mt.r, rstd.r], [tm.r])
                    stt("dve", x_.t[:, c, :], tm.t[:], pv(G2, l, c, j), x_.t[:, c, :], ALU.mult, ALU.add, [tm.r, G2.r, x_.r], [x_.r])
                if last and ti >= 1:
                    P.dma(out_d[:, t0 - TC:t0 - TC + N].rearrange("(c p) t -> p c t", p=128), x_.t[:], reads=[x_.r])
                else:
                    P.dma(res_d[:, t0:t0 + N].rearrange("(c p) t -> p c t", p=128), x_.t[:], reads=[x_.r, R_res], writes=[R_res])
            P.flush()
        P.barrier()
        if stop == ("D", l): return finish()

    return finish()


def _bf(a):
    return np.ascontiguousarray(a.astype(np.float32)).astype(ml_dtypes.bfloat16)

def _consts():
    ones = np.ones((128, 128), np.float32)
    ident = np.eye(128, dtype=np.float32)
    c = np.arange(128)
    same = (c[:, None] // 64) == (c[None, :] // 64)
    ang = 2 * np.pi * ((c[:, None] % 64) * (c[None, :] % 64)) / 64.0
    CDm = np.concatenate([np.where(same, np.cos(ang), 0.0), np.where(same, -np.sin(ang), 0.0)], axis=1)
    t1 = np.arange(64)
    a64 = 2 * np.pi * np.outer(t1, t1) / 64.0
    Cw, Sw = np.cos(a64), np.sin(a64)
    W1 = np.block([[Cw, -Sw], [Sw, Cw]])
    t2 = np.arange(64)[:, None, None]; f1 = np.arange(64)[None, :, None]; f2 = np.arange(64)[None, None, :]
    th = 2 * np.pi * (t2 * f1 / 4096.0 + t2 * f2 / 64.0)
    sc = 1.0 / np.sqrt(4096.0 * 64.0)
    M3 = np.concatenate([np.cos(th), np.sin(th)], axis=0) * sc
    t = np.arange(256)
    a256 = 2 * np.pi * np.outer(t, t) / 256.0
    sc2 = 1.0 / np.sqrt(256.0 * 64.0)
    C2 = (np.cos(a256) * sc2).reshape(2, 128, 256).transpose(1, 0, 2).reshape(128, 512)
    S2 = (np.sin(a256) * sc2).reshape(2, 128, 256).transpose(1, 0, 2).reshape(128, 512)
    cbf = np.concatenate([ones, ident, CDm, W1, C2, S2], axis=1)
    j = np.arange(128)[:, None]; i = np.arange(128)[None, :]
    cf = np.concatenate([(i >= j), (i <= j)], axis=1).astype(np.float32)
    return _bf(cbf), np.ascontiguousarray(cf), _bf(M3.reshape(128, 4096))

def _smalls(inp):
    sm = np.zeros((128, NS), np.float32)
    def put(name, arr):
        arr = np.asarray(arr, np.float32)
        sm[:, _off[name]:_off[name] + arr.shape[1]] = arr
    put("b_ada", np.asarray(inp["b_ada"]).reshape(L, 6, 8, 128).transpose(3, 0, 1, 2).reshape(128, L * 48))
    gains = np.stack([inp["g_pre_mix"], inp["g_post_mix"], inp["g_pre_ffn"], inp["g_post_ffn"]], axis=1)
    put("gain", gains.reshape(L, 4, 8, 128).transpose(3, 0, 1, 2).reshape(128, L * 32))
    put("g_gla", np.asarray(inp["g_gla"]).T)
    put("w_conv", np.asarray(inp["w_conv"]).reshape(L, 4, 2, 128).transpose(3, 0, 2, 1).reshape(128, L * 8))
    put("b_conv", np.asarray(inp["b_conv"]).reshape(L, 2, 128).transpose(2, 0, 1).reshape(128, L * 2))
    for nm in ("b_rg_a", "b_rg_x"):
        put(nm, np.asarray(inp[nm]).reshape(L, 2, 2, 128).transpose(3, 0, 1, 2).reshape(128, L * 4))
    put("lam", np.asarray(inp["rg_lam"]).reshape(L, 2, 2, 128).transpose(3, 0, 1, 2).reshape(128, L * 4))
    put("b_dec", np.asarray(inp["b_dec"]).reshape(L, 2, 2, 128).transpose(3, 0, 1, 2).reshape(128, L * 4))
    return sm

def _wrg(inp):
    w = np.zeros((L, 2, 2, 2, 128, 128), np.float32)
    for gi, nm in enumerate(("w_rg_a", "w_rg_x")):
        a = np.asarray(inp[nm], np.float32)
        for ch in range(2):
            for hh in range(2):
                w[:, :, gi, ch, hh * 64:(hh + 1) * 64, hh * 64:(hh + 1) * 64] = a[:, :, 2 * ch + hh]
    return w

def make_in_maps(inp):
    cbf, cf, m3 = _consts()
    sm = _smalls(inp)
    wrg = _wrg(inp)
    f = lambda k: np.ascontiguousarray(np.asarray(inp[k], np.float32))
    shared = {"smalls": sm, "w_ada": f("w_ada"), "w_in": f("w_in"), "w_out": f("w_out"), "w_ffn_gate": f("w_ffn_gate"),
              "w_ffn_up": f("w_ffn_up"), "w_ffn_down": f("w_ffn_down"), "w_dec": f("w_dec"), "wrg": wrg, "cbf": cbf, "cf32": cf, "m3": m3}
    x = np.asarray(inp["x"], np.float32); ctx = np.asarray(inp["ctx"], np.float32)
    c = np.asarray(inp["c"], np.float32); c_ctx = np.asarray(inp["c_ctx"], np.float32)
    maps = []
    for b in range(8):
        xT = np.ascontiguousarray(np.concatenate([ctx[b].T, x[b].T], axis=1))
        cc = np.stack([c[b], c_ctx], axis=1).reshape(8, 128, 2).transpose(1, 0, 2).reshape(128, 16)
        m = dict(shared); m["xT"] = xT; m["cc"] = np.ascontiguousarray(cc)
        maps.append(m)
    return maps

_NC = {}
def kernel(**inputs):
    if "nc" not in _NC:
        _NC["nc"] = build()
    maps = make_in_maps(inputs)
    res = run_bass_kernel_spmd(_NC["nc"], maps, core_ids=list(range(8)))
    out = np.stack([np.asarray(res.results[b]["outT"], np.float32).T for b in range(8)], axis=0)
    return np.ascontiguousarray(out)
```

```python
import contextlib
import numpy as np
import ml_dtypes
import concourse.bass as bass
import concourse.mybir as mybir
from concourse.bass_utils import run_bass_kernel_spmd

F32 = mybir.dt.float32
BF16 = mybir.dt.bfloat16
AF = mybir.ActivationFunctionType
ALU = mybir.AluOpType

D = 1024; TL = 4096; TC = 256; NT = TL + TC; L = 4
NIN = 2336; DFF = 2816; NCH = NT // 128
EPS = 1e-6
TT = [(0, 256)] + [(256 + 512 * i, 512) for i in range(8)]
TT256 = [(256 * i, 256) for i in range(17)]
FMCOLS = [0, 128, 256, 384, 1056, 1184, 1312, 1440, 1568, 1696, 1824, 1952, 2080, 2208]
ZQ, ZK, ZOG, ZF, ZRX, ZRG = 0, 2, 4, 8, 10, 12

_off = {}
_ns = 0
def _reg_small(name, n):
    global _ns
    _off[name] = _ns; _ns += n
_reg_small("b_ada", L * 48); _reg_small("gain", L * 32); _reg_small("g_gla", L)
_reg_small("w_conv", L * 8); _reg_small("b_conv", L * 2); _reg_small("b_rg_a", L * 4)
_reg_small("b_rg_x", L * 4); _reg_small("lam", L * 4); _reg_small("b_dec", L * 4)
NS = _ns


class Reg:
    __slots__ = ("w", "rl", "rd", "noread")
    def __init__(self):
        self.w = None; self.rl = {}; self.rd = []; self.noread = False

class Ins:
    __slots__ = ("eng", "fn", "deps", "need", "stream", "ticket", "is_dma")
    def __init__(self, eng, fn):
        self.eng = eng; self.fn = fn; self.deps = []; self.need = False
        self.stream = None; self.ticket = None; self.is_dma = False

ENGS = ("pe", "act", "dve", "pool", "sp")
NDMASEM = 8

class Prog:
    def __init__(self, nc, es):
        self.nc = nc
        self.q = {e: [] for e in ENGS}
        self.ndma = {e: 0 for e in ENGS}
        self.dma_last = {}
        self.cnt = {e: 0 for e in ENGS}
        self.seen = {e: {} for e in ENGS}
        self.last = {e: None for e in ENGS}
        self.regs = []
        self.bar = None
        self.sems = {}
        for e in ENGS:
            self.sems[e] = es.enter_context(nc.semaphore("s_" + e))
        for k in range(NDMASEM):
            self.sems[("sp", k)] = es.enter_context(nc.semaphore("d_sp%d" % k))

    def reg(self, persist=False):
        r = Reg()
        if persist: self.regs.append(r)
        return r

    def _track(self, ins, reads, writes, acc):
        deps = ins.deps
        for t in reads:
            if t.w is not None: deps.append(t.w)
        for t in writes:
            if t.w is not None: deps.append(t.w)
            deps.extend(t.rl.values()); deps.extend(t.rd)
        for t in reads:
            if t.noread: continue
            if ins.is_dma: t.rd.append(ins)
            else: t.rl[ins.eng] = ins
        for t in writes:
            t.w = ins; t.rl = {}; t.rd = []
        for t in acc:
            t.w = ins
        if self.bar is not None and self.bar[0].get(ins.eng):
            deps.extend(self.bar[1]); self.bar[0][ins.eng] = False
        for d in deps: d.need = True

    def op(self, eng, fn, reads=(), writes=(), acc=()):
        ins = Ins(eng, fn)
        self._track(ins, reads, writes, acc)
        self.q[eng].append(ins)
        return ins

    def dma(self, out, in_, reads=(), writes=(), eng="sp", **kw):
        ins = Ins(eng, lambda e: e.dma_start(out=out, in_=in_, **kw))
        ins.is_dma = True
        i = self.ndma[eng]; self.ndma[eng] += 1
        ins.stream = (eng, i % NDMASEM); ins.ticket = 16 * (i // NDMASEM + 1)
        prev = self.dma_last.get(ins.stream)
        if prev is not None: ins.deps.append(prev)
        self.dma_last[ins.stream] = ins
        ins.need = True
        self._track(ins, reads, writes, ())
        self.q[eng].append(ins)
        return ins

    def barrier(self):
        pend = [x for x in self.last.values() if x is not None] + list(self.dma_last.values())
        for e in ENGS:
            if self.q[e]: pend.append(self.q[e][-1])
        for x in pend: x.need = True
        self.bar = ({e: True for e in ENGS}, pend)

    def flush(self, final=False):
        nc = self.nc
        for t in self.regs:
            if t.w is not None: t.w.need = True
            for r in t.rl.values(): r.need = True
            for r in t.rd: r.need = True
        for e in ENGS:
            if self.q[e]: self.q[e][-1].need = True
        for e in ENGS:
            for ins in self.q[e]:
                if ins.is_dma or ins.ticket is not None: continue
                ins.stream = e
                if ins.need:
                    self.cnt[e] += 1; ins.ticket = self.cnt[e]
        sems = self.sems
        def run(e, eng):
            seen = self.seen[e]
            for ins in self.q[e]:
                need = {}
                for d in ins.deps:
                    if d.ticket is None:
                        raise RuntimeError("dep without ticket")
                    if d.ticket > need.get(d.stream, 0): need[d.stream] = d.ticket
                for s, v in need.items():
                    if seen.get(s, 0) >= v: continue
                    eng.wait_ge(sems[s], v); seen[s] = v
                r = ins.fn(eng)
                if ins.need:
                    r.then_inc(sems[ins.stream], 16 if ins.is_dma else 1)
            if final:
                for s, lastd in self.dma_last.items():
                    if s[0] == e and seen.get(s, 0) < lastd.ticket:
                        eng.wait_ge(sems[s], lastd.ticket); seen[s] = lastd.ticket
        with nc.Block() as block:
            if self.q["sp"]:
                @block.sync
                def _(eng): run("sp", eng)
            if self.q["pe"]:
                @block.tensor
                def _(eng): run("pe", eng)
            if self.q["act"]:
                @block.scalar
                def _(eng): run("act", eng)
            if self.q["dve"]:
                @block.vector
                def _(eng): run("dve", eng)
            if self.q["pool"]:
                @block.gpsimd
                def _(eng): run("pool", eng)
        for e in ENGS:
            if self.q[e]: self.last[e] = self.q[e][-1]
            self.q[e] = []


class Tile:
    def __init__(self, t, r):
        self.t = t; self.r = r


def build(n_layers=L, dump=(), stop=None):
    nc = bass.Bass("TRN2", target_bir_lowering=False)
    top = contextlib.ExitStack()
    P = Prog(nc, top)

    def din(name, shape, dt=F32):
        return nc.dram_tensor(name, shape, dt, kind="ExternalInput").ap()
    def dscr(name, shape, dt):
        kind = "ExternalOutput" if name in dump else "Internal"
        return nc.dram_tensor(name, shape, dt, kind=kind).ap()

    xT_d = din("xT", [D, NT]); cc_d = din("cc", [128, 16]); sm_d = din("smalls", [128, NS])
    w_ada_d = din("w_ada", [L, D, 6 * D]); w_in_d = din("w_in", [L, D, NIN]); w_out_d = din("w_out", [L, D, D])
    w_g_d = din("w_ffn_gate", [L, D, DFF]); w_u_d = din("w_ffn_up", [L, D, DFF]); w_d_d = din("w_ffn_down", [L, DFF, D])
    w_dec_d = din("w_dec", [L, 2, 16, 256]); wrg_d = din("wrg", [L, 2, 2, 2, 128, 128])
    NCB = 128 * 2 + 256 + 128 + 4 * 256
    cb_d = din("cbf", [128, NCB], BF16)
    m3_d = din("m3", [128, 4096], BF16)
    cf_d = din("cf32", [128, 256])
    out_d = nc.dram_tensor("outT", [D, TL], F32, kind="ExternalOutput").ap()

    res_d = dscr("res", [D, NT], F32)
    zT_d = dscr("zT", [14 * 128, NT], BF16)
    dec_d = dscr("decT", [32, NT], F32)
    vtok_d = dscr("vtok", [NT, 512], BF16)
    yT_d = dscr("yT", [D, NT], BF16)
    Yd_d = dscr("Yd", [TL, 512], BF16)
    Ad_d = dscr("Ad", [128, 64 * 256], BF16)
    R_res, R_z, R_dec, R_v, R_y, R_Yd, R_Ad = (P.reg(True) for _ in range(7))

    uid = {"i": 0}
    def SB(es, name, shape, dt=F32):
        uid["i"] += 1
        return es.enter_context(nc.sbuf_tensor("%s_%d" % (name, uid["i"]), shape, dt))
    def mk(es, name, shape, dt=F32, persist=False):
        return Tile(SB(es, name, shape, dt), P.reg(persist))

    psb = [Tile(top.enter_context(nc.psum_tensor("ps%d" % i, [128, 512], F32)), P.reg(True)) for i in range(6)]
    psM = Tile(top.enter_context(nc.psum_tensor("psM", [128, 512], F32)), P.reg(True))
    psT = Tile(top.enter_context(nc.psum_tensor("psT", [128, 1024], BF16)), P.reg(True))
    pstate = {"i": 0}
    def psum():
        p = psb[pstate["i"] % 6]; pstate["i"] += 1; return p

    cb = mk(top, "cb", [128, NCB], BF16, True)
    cf = mk(top, "cf", [128, 256], F32, True)
    sm = mk(top, "sm", [128, NS], F32, True)
    P.dma(cb.t[:], cb_d, writes=[cb.r]); P.dma(cf.t[:], cf_d, writes=[cf.r]); P.dma(sm.t[:], sm_d, writes=[sm.r])
    ones_b = cb.t[:, 0:128]; ident_b = cb.t[:, 128:256]; CD = cb.t[:, 256:512]; W1 = cb.t[:, 512:640]
    C256 = cb.t[:, 640:640 + 512].rearrange("p (a b) -> p a b", a=2)
    S256 = cb.t[:, 1152:1152 + 512].rearrange("p (a b) -> p a b", a=2)
    maskFB = cf.t[:, 0:256]
    def smv(name, idx, n=1):
        o = _off[name] + idx
        return sm.t[:, o:o + n]

    mod = mk(top, "mod", [128, L * 96], F32, True)
    A1 = mk(top, "A1", [128, L * 16], F32, True); G1 = mk(top, "G1", [128, L * 16], F32, True)
    A2 = mk(top, "A2", [128, L * 16], F32, True); G2 = mk(top, "G2", [128, L * 16], F32, True)
    cl = mk(top, "cl", [128, L * 4], F32, True); cl2 = mk(top, "cl2", [128, L * 4], F32, True); nbdec = mk(top, "nbdec", [128, L * 4], F32, True)
    def modv(l, g, fc, j):
        o = l * 96 + g * 16 + fc * 2 + j
        return mod.t[:, o:o + 1]
    def pv(tile_, l, fc, j):
        o = l * 16 + fc * 2 + j
        return tile_.t[:, o:o + 1]

    def act(out, in_, func, reads, writes, scale=1.0, bias=0.0, eng="act"):
        return P.op("act", lambda e: e.activation(out=out, in_=in_, func=func, scale=scale, bias=bias), reads=reads, writes=writes)
    def tt(eng, out, a, b, op, reads, writes):
        return P.op(eng, lambda e: e.tensor_tensor(out=out, in0=a, in1=b, op=op), reads=reads, writes=writes)
    def ts(eng, out, a, s1, s2, op0, op1, reads, writes):
        if s2 is None:
            return P.op(eng, lambda e: e.tensor_scalar(out=out, in0=a, scalar1=s1, scalar2=None, op0=op0), reads=reads, writes=writes)
        return P.op(eng, lambda e: e.tensor_scalar(out=out, in0=a, scalar1=s1, scalar2=s2, op0=op0, op1=op1), reads=reads, writes=writes)
    def stt(eng, out, a, s, b, op0, op1, reads, writes):
        return P.op(eng, lambda e: e.scalar_tensor_tensor(out=out, in0=a, scalar=s, in1=b, op0=op0, op1=op1), reads=reads, writes=writes)
    def cp(eng, out, in_, reads, writes):
        if eng == "act":
            return P.op("act", lambda e: e.activation(out=out, in_=in_, func=AF.Copy), reads=reads, writes=writes)
        return P.op(eng, lambda e: e.tensor_copy(out=out, in_=in_), reads=reads, writes=writes)
    def mmx(ps, out, lhsT, rhs, start, stop, reads, newtile):
        if newtile:
            return P.op("pe", lambda e: e.matmul(out, lhsT=lhsT, rhs=rhs, start=start, stop=stop), reads=reads, writes=[ps.r])
        return P.op("pe", lambda e: e.matmul(out, lhsT=lhsT, rhs=rhs, start=start, stop=stop), reads=reads, acc=[ps.r])
    def mm(ps, out, lhsT, rhs, first, last, reads):
        return mmx(ps, out, lhsT, rhs, first, last, reads, first)

    evs = {"i": 0}
    def evac(out, in_, reads, writes):
        evs["i"] += 1
        return cp("act" if evs["i"] % 2 else "dve", out, in_, reads, writes)

    def rstd_from_ps(ps, N, lnv, rstd, inv_n):
        act(lnv.t[:, :N], ps.t[:, :N], AF.Ln, [ps.r], [lnv.r], scale=inv_n, bias=EPS)
        act(rstd.t[:, :N], lnv.t[:, :N], AF.Exp, [lnv.r], [rstd.r], scale=-0.5)

    class WLoader:
        def __init__(self, es, ncols=1024, engs=("act", "dve"), nbuf=2):
            self.st = [mk(es, "wst%d" % i, [128, ncols]) for i in range(nbuf)]
            self.i = 0; self.nc = ncols; self.engs = engs; self.nb = nbuf
        def piece(self, dst, dreg, src, n, eng=None):
            st = self.st[self.i % self.nb]
            e = eng or self.engs[self.i % len(self.engs)]
            self.i += 1
            P.dma(st.t[:, :n], src, writes=[st.r])
            cp(e, dst, st.t[:, :n], [st.r], [dreg])
        def load(self, dst, dreg, src, ncols, eng=None):
            for c0 in range(0, ncols, self.nc):
                n = min(self.nc, ncols - c0)
                self.piece(dst[:, c0:c0 + n], dreg, src[:, c0:c0 + n], n, eng)

    modst = {}
    def emit_mod_groups(l, wst, groups):
        ps = psM
        for g in groups:
            w = wst[modst.setdefault("k", 0) % 2]; modst["k"] += 1
            P.dma(w.t[:], w_ada_d[l, :, g * 1024:(g + 1) * 1024].rearrange("(kc p) n -> p kc n", p=128), writes=[w.r])
            for fc in range(8):
                o = (g * 8 + fc) * 2
                for kc in range(8):
                    mmx(ps, ps.t[:, o:o + 2], w.t[:, kc, fc * 128:(fc + 1) * 128], modst["sv"].t[:, kc * 2:kc * 2 + 2],
                        kc == 0, kc == 7, [w.r, modst["sv"].r], (g == 0 and fc == 0 and kc == 0))
    def emit_mod_finish(l):
        ps = psM
        mv = mod.t[:, l * 96:(l + 1) * 96].rearrange("p (a j) -> p a j", j=2)
        pv2 = ps.t[:, 0:96].rearrange("p (a j) -> p a j", j=2)
        for j in range(2):
            tt("dve", mv[:, :, j], pv2[:, :, j], smv("b_ada", l * 48, 48), ALU.add, [ps.r, sm.r], [mod.r])
        for (dst, gsc, gidx) in ((A1, 1, 0), (A2, 4, 2)):
            dv = dst.t[:, l * 16:(l + 1) * 16].rearrange("p (a j) -> p a j", j=2)
            scv = mod.t[:, l * 96 + gsc * 16: l * 96 + gsc * 16 + 16].rearrange("p (a j) -> p a j", j=2)
            for j in range(2):
                stt("dve", dv[:, :, j], scv[:, :, j], 1.0, smv("gain", l * 32 + gidx * 8, 8), ALU.add, ALU.mult, [mod.r, sm.r], [dst.r])
        for (dst, ggt, gidx) in ((G1, 2, 1), (G2, 5, 3)):
            dv = dst.t[:, l * 16:(l + 1) * 16].rearrange("p (a j) -> p a j", j=2)
            gv = mod.t[:, l * 96 + ggt * 16: l * 96 + ggt * 16 + 16].rearrange("p (a j) -> p a j", j=2)
            for j in range(2):
                tt("dve", dv[:, :, j], gv[:, :, j], smv("gain", l * 32 + gidx * 8, 8), ALU.mult, [mod.r, sm.r], [dst.r])

    sv = mk(top, "sv", [128, 16], F32, True)
    modst["sv"] = sv
    with contextlib.ExitStack() as ph:
        cc = mk(ph, "cc_sb", [128, 16])
        P.dma(cc.t[:], cc_d, writes=[cc.r])
        act(sv.t[:], cc.t[:], AF.Silu, [cc.r], [sv.r])
        wst0 = [mk(ph, "wada%d" % i, [128, 8, 1024]) for i in range(2)]
        emit_mod_groups(0, wst0, range(6))
        emit_mod_finish(0)
        tmp = mk(ph, "tmpl", [128, L * 4])
        act(tmp.t[:], smv("lam", 0, L * 4), AF.Exp, [sm.r], [tmp.r], scale=-1.0)
        act(tmp.t[:], tmp.t[:], AF.Ln, [tmp.r], [tmp.r], scale=1.0, bias=1.0)
        ts("dve", cl.t[:], tmp.t[:], -8.0, None, ALU.mult, None, [tmp.r], [cl.r])
        ts("dve", cl2.t[:], tmp.t[:], -16.0, None, ALU.mult, None, [tmp.r], [cl2.r])
        ts("dve", nbdec.t[:], smv("b_dec", 0, L * 4), -1.0, None, ALU.mult, None, [sm.r], [nbdec.r])
        P.flush()
    P.barrier()
    for t_ in (cb, cf, sm, mod, A1, G1, A2, G2, cl, cl2, nbdec, sv):
        t_.r.noread = True

    def finish():
        P.flush(final=True)
        top.close()
        return nc

    for l in range(n_layers):
        src_d = xT_d if l == 0 else res_d
        src_reads = [] if l == 0 else [R_res]
        with contextlib.ExitStack() as ph:
            Win = mk(ph, "Win", [128, 8, NIN], BF16)
            wl = WLoader(ph, 1168, ("act", "dve"))
            for kc in range(8):
                wl.load(Win.t[:, kc, :], Win.r, w_in_d[l, kc * 128:(kc + 1) * 128, :], NIN)
            xs = [mk(ph, "xs%d" % i, [128, 8, 512]) for i in range(2)]
            sq = mk(ph, "sq", [128, 8, 512], BF16)
            h = mk(ph, "h", [128, 8, 512], BF16)
            lnv = mk(ph, "lnv", [128, 512]); rstd = mk(ph, "rstd", [128, 512])
            tmp = [mk(ph, "tmp%d" % i, [128, 512]) for i in range(2)]
            zst = [mk(ph, "zst%d" % i, [128, 14, 512], BF16) for i in range(2)]
            vst = mk(ph, "vst", [128, 4, 512], BF16)
            dst_ = mk(ph, "dst", [32, 512])
            def load_x(ti):
                t0, N = TT[ti]
                x_ = xs[ti % 2]
                P.dma(x_.t[:, :, :N], src_d[:, t0:t0 + N].rearrange("(c p) t -> p c t", p=128), reads=src_reads, writes=[x_.r])
            load_x(0)
            for ti, (t0, N) in enumerate(TT):
                j = 1 if ti == 0 else 0
                if ti + 1 < len(TT): load_x(ti + 1)
                x_ = xs[ti % 2]
                tt("pool", sq.t[:, :, :N], x_.t[:, :, :N], x_.t[:, :, :N], ALU.mult, [x_.r], [sq.r])
                ps = psum()
                for c in range(8):
                    mm(ps, ps.t[:, :N], ones_b, sq.t[:, c, :N], c == 0, c == 7, [cb.r, sq.r])
                rstd_from_ps(ps, N, lnv, rstd, 1.0 / D)
                for c in range(8):
                    tm = tmp[c % 2]
                    tt("dve", tm.t[:, :N], x_.t[:, c, :N], rstd.t[:, :N], ALU.mult, [x_.r, rstd.r], [tm.r])
                    act(h.t[:, c, :N], tm.t[:, :N], AF.Identity, [tm.r, A1.r, mod.r], [h.r], scale=pv(A1, l, c, j), bias=modv(l, 0, c, j))
                zs = zst[ti % 2]
                for ci, col in enumerate(FMCOLS):
                    ps = psum()
                    for kc in range(8):
                        mm(ps, ps.t[:, :N], Win.t[:, kc, col:col + 128], h.t[:, kc, :N], kc == 0, kc == 7, [Win.r, h.r])
                    evac(zs.t[:, ci, :N], ps.t[:, :N], [ps.r], [zs.r])
                P.dma(zT_d[:, t0:t0 + N].rearrange("(c p) t -> p c t", p=128), zs.t[:, :, :N], reads=[zs.r], writes=[R_z])
                ps = psum()
                for kc in range(8):
                    mm(ps, ps.t[0:32, :N], Win.t[:, kc, 1024:1056], h.t[:, kc, :N], kc == 0, kc == 7, [Win.r, h.r])
                evac(dst_.t[:, :N], ps.t[0:32, :N], [ps.r], [dst_.r])
                P.dma(dec_d[:, t0:t0 + N], dst_.t[:, :N], reads=[dst_.r], writes=[R_dec])
                ns = N // 128
                for s in range(ns):
                    ps = psum()
                    for kc in range(8):
                        mm(ps, ps.t[:, :512], h.t[:, kc, s * 128:(s + 1) * 128], Win.t[:, kc, 512:1024], kc == 0, kc == 7, [Win.r, h.r])
                    evac(vst.t[:, s, :], ps.t[:, :512], [ps.r], [vst.r])
                P.dma(vtok_d[t0:t0 + N, :].rearrange("(s p) v -> p s v", p=128), vst.t[:, :ns, :], reads=[vst.r], writes=[R_v])
            P.flush()
        P.barrier()
        if stop == ("A", l): return finish()

        with contextlib.ExitStack() as ph:
            S = [mk(ph, "S%d" % i, [128, NT]) for i in range(6)]
            H = [mk(ph, "H%d" % i, [128, NT], BF16) for i in range(3)]
            wst = mk(ph, "wrgst", [128, 4, 128]); wb = mk(ph, "wrgb", [128, 4, 128], BF16)
            nxt_mod = (l + 1 < n_layers)
            if nxt_mod:
                wstm = [mk(ph, "wadaB%d" % i, [128, 8, 1024]) for i in range(2)]
            for ch in range(2):
                P.dma(H[0].t[:], zT_d[(ZRX + ch) * 128:(ZRX + ch + 1) * 128, :], reads=[R_z], writes=[H[0].r])
                P.dma(wst.t[:], wrg_d[l, :, :, ch].rearrange("d g k m -> k (d g) m"), writes=[wst.r])
                cp("pool", wb.t[:], wst.t[:], [wst.r], [wb.r])
                z = H[0]; u = S[0]
                wc = lambda tap: smv("w_conv", l * 8 + ch * 4 + tap)
                ts("dve", u.t[:], z.t[:], wc(1), smv("b_conv", l * 2 + ch), ALU.mult, ALU.add, [z.r, sm.r], [u.r])
                for (lo, n, rows, w) in ((0, TC, 1, TC), (TC, TL, 64, 64)):
                    zv = z.t[:, lo:lo + n].rearrange("p (r w) -> p r w", w=w)
                    uv = u.t[:, lo:lo + n].rearrange("p (r w) -> p r w", w=w)
                    stt("dve", uv[:, :, 1:w], zv[:, :, 0:w - 1], wc(0), uv[:, :, 1:w], ALU.mult, ALU.add, [z.r, sm.r, u.r], [u.r])
                    stt("dve", uv[:, :, 0:w - 1], zv[:, :, 1:w], wc(2), uv[:, :, 0:w - 1], ALU.mult, ALU.add, [z.r, sm.r, u.r], [u.r])
                    stt("dve", uv[:, :, 0:w - 2], zv[:, :, 2:w], wc(3), uv[:, :, 0:w - 2], ALU.mult, ALU.add, [z.r, sm.r, u.r], [u.r])
                ub = H[1]
                cp("act", ub.t[:], u.t[:], [u.r], [ub.r])
                for d in range(2):
                    r_, i_, a_ = S[1], S[2], S[3]
                    hd = S[4 + d]
                    for (gate, dstt, bname) in ((0, r_, "b_rg_a"), (1, i_, "b_rg_x")):
                        for (t0, N) in TT:
                            ps = psum()
                            mm(ps, ps.t[:, :N], wb.t[:, d * 2 + gate, :], ub.t[:, t0:t0 + N], True, True, [wb.r, ub.r])
                            act(dstt.t[:, t0:t0 + N], ps.t[:, :N], AF.Sigmoid, [ps.r, sm.r], [dstt.r], bias=smv(bname, l * 4 + d * 2 + ch))
                    o4 = l * 4 + d * 2 + ch
                    act(a_.t[:], r_.t[:], AF.Exp, [r_.r, cl.r], [a_.r], scale=cl.t[:, o4:o4 + 1])
                    act(r_.t[:], r_.t[:], AF.Exp, [r_.r, cl2.r], [r_.r], scale=cl2.t[:, o4:o4 + 1])
                    ts("dve", r_.t[:], r_.t[:], -1.0, 1.0, ALU.mult, ALU.add, [r_.r], [r_.r])
                    act(r_.t[:], r_.t[:], AF.Sqrt, [r_.r], [r_.r])
                    tt("pool", i_.t[:], i_.t[:], u.t[:], ALU.mult, [i_.r, u.r], [i_.r])
                    tt("dve", r_.t[:], r_.t[:], i_.t[:], ALU.mult, [r_.r, i_.r], [r_.r])
                    if d == 0:
                        P.op("dve", lambda e, hd=hd, a_=a_, r_=r_: e.tensor_tensor_scan(out=hd.t[:, 0:TC], data0=a_.t[:, 0:TC], data1=r_.t[:, 0:TC], initial=0.0, op0=ALU.mult, op1=ALU.add),
                             reads=[a_.r, r_.r], writes=[hd.r])
                        P.op("dve", lambda e, hd=hd, a_=a_, r_=r_: e.tensor_tensor_scan(out=hd.t[:, TC:NT], data0=a_.t[:, TC:NT], data1=r_.t[:, TC:NT], initial=hd.t[:, TC - 1:TC], op0=ALU.mult, op1=ALU.add),
                             reads=[a_.r, r_.r, hd.r], writes=[hd.r])
                    else:
                        P.op("dve", lambda e, hd=hd, a_=a_, r_=r_: e.tensor_tensor_scan(out=hd.t[:, 0:TC][:, ::-1], data0=a_.t[:, 0:TC][:, ::-1], data1=r_.t[:, 0:TC][:, ::-1], initial=0.0, op0=ALU.mult, op1=ALU.add),
                             reads=[a_.r, r_.r], writes=[hd.r])
                        P.op("dve", lambda e, hd=hd, a_=a_, r_=r_: e.tensor_tensor_scan(out=hd.t[:, TC:NT][:, ::-1], data0=a_.t[:, TC:NT][:, ::-1], data1=r_.t[:, TC:NT][:, ::-1], initial=hd.t[:, 0:1], op0=ALU.mult, op1=ALU.add),
                             reads=[a_.r, r_.r, hd.r], writes=[hd.r])
                tt("pool", S[4].t[:], S[4].t[:], S[5].t[:], ALU.add, [S[4].r, S[5].r], [S[4].r])
                P.dma(H[0].t[:], zT_d[(ZRG + ch) * 128:(ZRG + ch + 1) * 128, :], reads=[R_z], writes=[H[0].r])
                act(S[5].t[:], H[0].t[:], AF.Gelu, [H[0].r], [S[5].r])
                tt("dve", H[2].t[:], S[4].t[:], S[5].t[:], ALU.mult, [S[4].r, S[5].r], [H[2].r])
                P.dma(yT_d[768 + ch * 128: 768 + (ch + 1) * 128, :], H[2].t[:], reads=[H[2].r], writes=[R_y])
                if nxt_mod:
                    emit_mod_groups(l + 1, wstm, range(3 * ch, 3 * ch + 3))
                    if ch == 1: emit_mod_finish(l + 1)
            P.flush()
        P.barrier()
        if stop == ("B1", l): return finish()

        with contextlib.ExitStack() as ph:
            mF = mk(ph, "mF", [128, NT + 1], BF16)
            P.op("pool", lambda e: e.memset(mF.t[:], 1.0), writes=[mF.r])
            P.op("pool", lambda e: e.memset(mF.t[:, 0:NT + 1:128], 0.0), writes=[mF.r])
            lrt = mk(ph, "lr", [48, NT])
            wdt = mk(ph, "wdec", [48, 256])
            for d in range(2):
                P.dma(lrt.t[d * 32:d * 32 + 16, :], dec_d[d * 16:(d + 1) * 16, :], reads=[R_dec], writes=[lrt.r])
                P.dma(wdt.t[d * 32:d * 32 + 16, :], w_dec_d[l, d], writes=[wdt.r])
            vtok = mk(ph, "vtok", [128, NCH, 256], BF16)
            Sa = mk(ph, "Sa", [128, NT])
            Hq = mk(ph, "Hq", [128, NT], BF16); Hk = mk(ph, "Hk", [128, NT], BF16)
            qt = [mk(ph, "qt%d" % d, [128, NT], BF16) for d in range(2)]
            kt = [mk(ph, "kt%d" % d, [128, NT], BF16) for d in range(2)]
            ktok = [mk(ph, "ktok%d" % d, [128, NCH, 128], BF16) for d in range(2)]
            ebc = [mk(ph, "ebc%d" % d, [128, NCH]) for d in range(2)]
            Sbf = [mk(ph, "Sbf%d" % d, [128, NCH, 256], BF16) for d in range(2)]
            Scur = [[mk(ph, "Sc%d%d" % (d, i), [128, 256]) for i in range(2)] for d in range(2)]
            kvs = [mk(ph, "kvs%d" % i, [128, 256]) for i in range(4)]
            am = [mk(ph, "am%d" % i, [128, 256], BF16) for i in range(4)]
            Hog = Hq; yg = Hq
            sqo = mk(ph, "sqo", [128, 512], BF16); lnv = mk(ph, "lnv2", [128, 512]); rstd = mk(ph, "rstd2", [128, 512])
            t1 = mk(ph, "t1", [128, 512]); t2 = mk(ph, "t2", [128, 512])
            order = [list(range(NCH)), [1, 0] + list(range(NCH - 1, 1, -1))]
            for hp in range(2):
                P.dma(Hq.t[:], zT_d[(ZQ + hp) * 128:(ZQ + hp + 1) * 128, :], reads=[R_z], writes=[Hq.r])
                P.dma(Hk.t[:], zT_d[(ZK + hp) * 128:(ZK + hp + 1) * 128, :], reads=[R_z], writes=[Hk.r])
                for q4 in range(2):
                    P.dma(vtok.t[:, q4 * 17:(q4 + 1) * 17, :], vtok_d[q4 * 17 * 128:(q4 + 1) * 17 * 128, hp * 256:(hp + 1) * 256].rearrange("(s p) v -> p s v", p=128), reads=[R_v], writes=[vtok.r])
                for d in range(2):
                    o4 = l * 4 + d * 2 + hp
                    for (t0, N) in TT:
                        ps = psum()
                        mm(ps, ps.t[:, :N], wdt.t[d * 32:d * 32 + 16, hp * 128:(hp + 1) * 128], lrt.t[d * 32:d * 32 + 16, t0:t0 + N], True, True, [wdt.r, lrt.r])
                        act(Sa.t[:, t0:t0 + N], ps.t[:, :N], AF.Exp, [ps.r, nbdec.r], [Sa.r], scale=-1.0, bias=nbdec.t[:, o4:o4 + 1])
                    act(Sa.t[:], Sa.t[:], AF.Ln, [Sa.r], [Sa.r], scale=1.0, bias=1.0)
                    if d == 0:
                        P.op("dve", lambda e: e.tensor_tensor_scan(out=Sa.t[:], data0=mF.t[:, 0:NT], data1=Sa.t[:], initial=0.0, op0=ALU.mult, op1=ALU.add),
                             reads=[mF.r, Sa.r], writes=[Sa.r])
                    else:
                        P.op("dve", lambda e: e.tensor_tensor_scan(out=Sa.t[:, ::-1], data0=mF.t[:, 1:NT + 1][:, ::-1], data1=Sa.t[:, ::-1], initial=0.0, op0=ALU.mult, op1=ALU.add),
                             reads=[mF.r, Sa.r], writes=[Sa.r])
                    for ei, (t0, N) in enumerate(TT):
                        te = (t1, t2)[ei % 2]
                        act(te.t[:, :N], Sa.t[:, t0:t0 + N], AF.Exp, [Sa.r], [te.r], scale=1.0 / 16.0)
                        tt("dve", kt[d].t[:, t0:t0 + N], Hk.t[:, t0:t0 + N], te.t[:, :N], ALU.mult, [Hk.r, te.r], [kt[d].r])
                    act(Sa.t[:], Sa.t[:], AF.Exp, [Sa.r], [Sa.r], scale=-1.0 / 16.0)
                    stt("dve", qt[d].t[:], Hq.t[:], 0.125, Sa.t[:], ALU.mult, ALU.mult, [Hq.r, Sa.r], [qt[d].r])
                    e_at = 127 if d == 0 else 0
                    cp("pool", ebc[d].t[:], Sa.t[:, e_at:NT:128], [Sa.r], [ebc[d].r])
                    for n0 in range(0, NCH, 4):
                        nn = min(4, NCH - n0)
                        for i in range(nn):
                            n = n0 + i
                            P.op("pe", lambda e, d=d, n=n, i=i: e.transpose(psT.t[:, i * 128:(i + 1) * 128], kt[d].t[:, n * 128:(n + 1) * 128], ident_b),
                                 reads=[kt[d].r, cb.r], **({"writes": [psT.r]} if i == 0 else {"acc": [psT.r]}))
                        evac(ktok[d].t[:, n0:n0 + nn, :], psT.t[:, :nn * 128].rearrange("p (a b) -> p a b", b=128), [psT.r], [ktok[d].r])
                cur = [0, 0]
                for d in range(2):
                    P.op("pool", lambda e, d=d: e.memset(Scur[d][0].t[:], 0.0), writes=[Scur[d][0].r])
                    P.op("pool", lambda e, d=d: e.memset(Sbf[d].t[:, order[d][0], :], 0.0), writes=[Sbf[d].r])
                kvi = 0
                for s in range(NCH - 1):
                    for d in range(2):
                        n = order[d][s]; nxt = order[d][s + 1]
                        ps = psum()
                        mm(ps, ps.t[:, :256], ktok[d].t[:, n, :], vtok.t[:, n, :], True, True, [ktok[d].r, vtok.r])
                        kv = kvs[kvi % 4]; kvi += 1
                        act(kv.t[:], ps.t[:, :256], AF.Identity, [ps.r, ebc[d].r], [kv.r], scale=ebc[d].t[:, n:n + 1])
                        sc_, sn_ = Scur[d][cur[d] % 2], Scur[d][(cur[d] + 1) % 2]; cur[d] += 1
                        stt("dve", sn_.t[:], sc_.t[:], ebc[d].t[:, n:n + 1], kv.t[:], ALU.mult, ALU.add, [sc_.r, ebc[d].r, kv.r], [sn_.r])
                        cp("pool" if d == 0 else "act", Sbf[d].t[:, nxt, :], sn_.t[:], [sn_.r], [Sbf[d].r])
                for hh in range(2):
                    head = 2 * hp + hh
                    Rw = slice(hh * 64, (hh + 1) * 64)
                    P.dma(Hog.t[:], zT_d[(ZOG + head) * 128:(ZOG + head + 1) * 128, :], reads=[R_z], writes=[Hog.r])
                    units = [(ti, ci) for ti, (t0, N) in enumerate(TT) for ci in range(N // 128)]
                    po_of = {}
                    st8 = {"ai": 0}
                    def S12(u, Rw=Rw):
                        ti, ci = u
                        n = TT[ti][0] // 128 + ci
                        cs = slice(n * 128, (n + 1) * 128)
                        pa = psb[2 + st8["ai"] % 3]
                        for d in range(2):
                            mmx(pa, pa.t[:, d * 128:(d + 1) * 128], kt[d].t[Rw, cs], qt[d].t[Rw, cs], True, True, [kt[d].r, qt[d].r], d == 0)
                        a_ = am[st8["ai"] % 4]; st8["ai"] += 1
                        tt("dve", a_.t[:], pa.t[:, :256], maskFB, ALU.mult, [pa.r, cf.r], [a_.r])
                        return a_
                    def S3(u, a_, Rw=Rw, hh=hh):
                        ti, ci = u
                        t0, N = TT[ti]
                        n = t0 // 128 + ci
                        cs = slice(n * 128, (n + 1) * 128)
                        if ci == 0: po_of[ti] = psb[ti % 2]
                        po = po_of[ti]
                        oc = po.t[:, ci * 128:(ci + 1) * 128]
                        vv = vtok.t[:, n, hh * 128:(hh + 1) * 128]
                        mmx(po, oc, vv, a_.t[:, 0:128], True, False, [vtok.r, a_.r], ci == 0)
                        mmx(po, oc, vv, a_.t[:, 128:256], False, False, [vtok.r, a_.r], False)
                        mmx(po, oc, Sbf[0].t[Rw, n, hh * 128:(hh + 1) * 128], qt[0].t[Rw, cs], False, False, [Sbf[0].r, qt[0].r], False)
                        mmx(po, oc, Sbf[1].t[Rw, n, hh * 128:(hh + 1) * 128], qt[1].t[Rw, cs], False, True, [Sbf[1].r, qt[1].r], False)
                        return ci == N // 128 - 1
                    def NORM(ti):
                        t0, N = TT[ti]
                        po = po_of[ti]
                        act(sqo.t[:, :N], po.t[:, :N], AF.Square, [po.r], [sqo.r])
                        pss = psb[5]
                        mm(pss, pss.t[:, :N], ones_b, sqo.t[:, :N], True, True, [cb.r, sqo.r])
                        rstd_from_ps(pss, N, lnv, rstd, 1.0 / 128.0)
                        tt("dve", t1.t[:, :N], po.t[:, :N], rstd.t[:, :N], ALU.mult, [po.r, rstd.r], [t1.r])
                        act(t2.t[:, :N], Hog.t[:, t0:t0 + N], AF.Silu, [Hog.r], [t2.r])
                        stt("dve", yg.t[:, t0:t0 + N], t1.t[:, :N], smv("g_gla", l), t2.t[:, :N], ALU.mult, ALU.mult, [t1.r, t2.r, sm.r], [yg.r])
                    pq = []; nq = []
                    def step3():
                        u0, a0 = pq.pop(0)
                        for e_ in nq: e_[1] -= 1
                        while nq and nq[0][1] <= 0:
                            NORM(nq.pop(0)[0])
                        if S3(u0, a0): nq.append([u0[0], 2])
                    for u in units:
                        pq.append((u, S12(u)))
                        if len(pq) > 2: step3()
                    while pq: step3()
                    while nq: NORM(nq.pop(0)[0])
                    P.dma(yT_d[head * 128:(head + 1) * 128, :], yg.t[:], reads=[yg.r], writes=[R_y])
            P.flush()
        P.barrier()
        if stop == ("B2", l): return finish()

        with contextlib.ExitStack() as ph:
            fT = [mk(ph, "fT%d" % kc, [128, NT], BF16) for kc in range(2)]
            for kc in range(2):
                P.dma(fT[kc].t[:], zT_d[(ZF + kc) * 128:(ZF + kc + 1) * 128, :], reads=[R_z], writes=[fT[kc].r])
            Yc = mk(ph, "Yc", [128, 2, 512], BF16)
            Yl = mk(ph, "Yl", [128, 32, 512], BF16)
            for s in range(NCH):
                ps = psum()
                for kc in range(2):
                    mmx(ps, ps.t[:, kc * 256:(kc + 1) * 256], fT[kc].t[:, s * 128:(s + 1) * 128], CD, True, True, [fT[kc].r, cb.r], kc == 0)
                dst = Yc if s < 2 else Yl
                si = s if s < 2 else s - 2
                evac(dst.t[:, si, :].rearrange("p (ri kc c) -> p kc ri c", ri=2, kc=2),
                     ps.t[:, :512].rearrange("p (kc ri c) -> p kc ri c", kc=2, ri=2), [ps.r], [dst.r])
            for q4 in range(4):
                P.dma(Yd_d[q4 * 1024:(q4 + 1) * 1024, :].rearrange("(s p) v -> p s v", p=128), Yl.t[:, q4 * 8:(q4 + 1) * 8, :], reads=[Yl.r], writes=[R_Yd])
            yf = [mk(ph, "yf%d" % kc, [128, NT], BF16) for kc in range(2)]
            for kc in range(2):
                ps = psum()
                k4 = 0
                for tch in range(2):
                    for (ri, tab) in ((0, C256), (1, S256)):
                        mm(ps, ps.t[:, :256], Yc.t[:, tch, ri * 256 + kc * 128: ri * 256 + (kc + 1) * 128], tab[:, tch, :], k4 == 0, k4 == 3, [Yc.r, cb.r])
                        k4 += 1
                evac(yf[kc].t[:, 0:TC], ps.t[:, :256], [ps.r], [yf[kc].r])
            L1 = mk(ph, "L1", [128, 64, 256], BF16)
            for ri in range(2):
                P.dma(L1.t[ri * 64:(ri + 1) * 64, :, :],
                      Yd_d[:, ri * 256:(ri + 1) * 256].rearrange("(t1 t2) c -> t1 t2 c", t2=64), reads=[R_Yd], writes=[L1.r])
            A_sb = mk(ph, "A_sb", [128, 64 * 256], BF16)
            L1f = L1.t[:].rearrange("p a b -> p (a b)")
            for i in range(32):
                ps = psum()
                mm(ps, ps.t[:, :512], W1, L1f[:, i * 512:(i + 1) * 512], True, True, [cb.r, L1.r])
                evac(A_sb.t[:, i * 512:(i + 1) * 512], ps.t[:, :512], [ps.r], [A_sb.r])
            P.dma(Ad_d, A_sb.t[:], reads=[A_sb.r], writes=[R_Ad])
            L3 = mk(ph, "L3", [128, 64, 256], BF16)
            m3 = mk(ph, "m3", [128, 4096], BF16)
            P.dma(m3.t[:], m3_d, writes=[m3.r])
            M3 = m3.t[:].rearrange("p (a b) -> p a b", a=64)
            for ri in range(2):
                P.dma(L3.t[ri * 64:(ri + 1) * 64, :, :],
                      Ad_d[ri * 64:(ri + 1) * 64, :].rearrange("f1 (t2 c) -> t2 f1 c", c=256), reads=[R_Ad], writes=[L3.r])
            for kc in range(2):
                yv = yf[kc].t[:, TC:NT].rearrange("p (f2 f1) -> p f1 f2", f1=64)
                for g8 in range(8):
                    ps = psum()
                    for i in range(8):
                        f1 = g8 * 8 + i
                        mmx(ps, ps.t[:, i * 64:(i + 1) * 64], L3.t[:, f1, kc * 128:(kc + 1) * 128], M3[:, f1, :], True, True, [L3.r, m3.r], i == 0)
                    evac(yv[:, g8 * 8:(g8 + 1) * 8, :], ps.t[:, :512].rearrange("p (a b) -> p a b", b=64), [ps.r], [yf[kc].r])
                P.dma(yT_d[512 + kc * 128:512 + (kc + 1) * 128, :], yf[kc].t[:], reads=[yf[kc].r], writes=[R_y])
            P.flush()
        P.barrier()
        if stop == ("B3", l): return finish()

        ffn = contextlib.ExitStack()
        Wg = mk(ffn, "Wg", [128, 8, DFF], BF16, True); Wu = mk(ffn, "Wu", [128, 8, DFF], BF16, True)
        with contextlib.ExitStack() as ph:
            Wo = mk(ph, "Wo", [128, 8, D], BF16)
            wl = WLoader(ph, 1024, ("act", "dve"))
            for kc in range(8):
                wl.load(Wo.t[:, kc, :], Wo.r, w_out_d[l, kc * 128:(kc + 1) * 128, :], D)
            pre = []
            for kc in range(8):
                for (W_, wd_) in ((Wg, w_g_d), (Wu, w_u_d)):
                    for c0 in (0, 1024, 2048):
                        n_ = min(1024, DFF - c0)
                        pre.append((W_.t[:, kc, c0:c0 + n_], W_.r, wd_[l, kc * 128:(kc + 1) * 128, c0:c0 + n_], n_))
            xs = [mk(ph, "xs%d" % i, [128, 8, 512]) for i in range(2)]
            ys = [mk(ph, "ys%d" % i, [128, 8, 512], BF16) for i in range(2)]
            mt = mk(ph, "mt", [128, 8, 512]); sq = mk(ph, "sq", [128, 8, 512], BF16)
            lnv = mk(ph, "lnv", [128, 512]); rstd = mk(ph, "rstd", [128, 512])
            tmp = [mk(ph, "tmp%d" % i, [128, 512]) for i in range(2)]
            def load_c(ti):
                t0, N = TT[ti]
                P.dma(xs[ti % 2].t[:, :, :N], src_d[:, t0:t0 + N].rearrange("(c p) t -> p c t", p=128), reads=src_reads, writes=[xs[ti % 2].r])
                P.dma(ys[ti % 2].t[:, :, :N], yT_d[:, t0:t0 + N].rearrange("(c p) t -> p c t", p=128), reads=[R_y], writes=[ys[ti % 2].r])
            load_c(0)
            for ti, (t0, N) in enumerate(TT):
                j = 1 if ti == 0 else 0
                if ti + 1 < len(TT): load_c(ti + 1)
                x_ = xs[ti % 2]; y_ = ys[ti % 2]
                for dc in range(8):
                    ps = psum()
                    for kc in range(8):
                        mm(ps, ps.t[:, :N], Wo.t[:, kc, dc * 128:(dc + 1) * 128], y_.t[:, kc, :N], kc == 0, kc == 7, [Wo.r, y_.r])
                    cp("act", mt.t[:, dc, :N], ps.t[:, :N], [ps.r], [mt.r])
                tt("pool", sq.t[:, :, :N], mt.t[:, :, :N], mt.t[:, :, :N], ALU.mult, [mt.r], [sq.r])
                ps = psum()
                for c in range(8):
                    mm(ps, ps.t[:, :N], ones_b, sq.t[:, c, :N], c == 0, c == 7, [cb.r, sq.r])
                rstd_from_ps(ps, N, lnv, rstd, 1.0 / D)
                for c in range(8):
                    tm = tmp[c % 2]
                    tt("dve", tm.t[:, :N], mt.t[:, c, :N], rstd.t[:, :N], ALU.mult, [mt.r, rstd.r], [tm.r])
                    stt("dve", x_.t[:, c, :N], tm.t[:, :N], pv(G1, l, c, j), x_.t[:, c, :N], ALU.mult, ALU.add, [tm.r, G1.r, x_.r], [x_.r])
                P.dma(res_d[:, t0:t0 + N].rearrange("(c p) t -> p c t", p=128), x_.t[:, :, :N], reads=[x_.r] + src_reads, writes=[R_res])
                for _ in range(6):
                    if pre: wl.piece(*pre.pop(0), eng="pool")
            while pre: wl.piece(*pre.pop(0), eng="pool")
            P.flush()
        P.barrier()
        if stop == ("C", l):
            ffn.close(); return finish()

        with contextlib.ExitStack() as ph:
            Wd = mk(ph, "Wd", [128, 22, D], BF16)
            wl = WLoader(ph, 1024, ("act", "dve"))
            for kc in range(22):
                wl.load(Wd.t[:, kc, :], Wd.r, w_d_d[l, kc * 128:(kc + 1) * 128, :], D)
            NB = 256
            xs = [mk(ph, "xs%d" % i, [128, 8, NB]) for i in range(2)]
            sq = mk(ph, "sq", [128, 8, NB], BF16); h = mk(ph, "h", [128, 8, NB], BF16)
            hid = mk(ph, "hid", [128, 22, NB], BF16); mt = mk(ph, "mt", [128, 8, NB])
            lnv = mk(ph, "lnv", [128, NB]); rstd = mk(ph, "rstd", [128, NB])
            tmp = [mk(ph, "tmp%d" % i, [128, NB]) for i in range(2)]
            sg = [mk(ph, "sg%d" % i, [128, NB]) for i in range(2)]
            last = (l == n_layers - 1)
            def load_d(ti):
                t0, N = TT256[ti]
                P.dma(xs[ti % 2].t[:], res_d[:, t0:t0 + N].rearrange("(c p) t -> p c t", p=128), reads=[R_res], writes=[xs[ti % 2].r])
            load_d(0)
            for ti, (t0, N) in enumerate(TT256):
                j = 1 if ti == 0 else 0
                if ti + 1 < len(TT256): load_d(ti + 1)
                x_ = xs[ti % 2]
                tt("pool", sq.t[:], x_.t[:], x_.t[:], ALU.mult, [x_.r], [sq.r])
                ps = psum()
                for c in range(8):
                    mm(ps, ps.t[:, :N], ones_b, sq.t[:, c, :], c == 0, c == 7, [cb.r, sq.r])
                rstd_from_ps(ps, N, lnv, rstd, 1.0 / D)
                for c in range(8):
                    tm = tmp[c % 2]
                    tt("dve", tm.t[:], x_.t[:, c, :], rstd.t[:], ALU.mult, [x_.r, rstd.r], [tm.r])
                    act(h.t[:, c, :], tm.t[:], AF.Identity, [tm.r, A2.r, mod.r], [h.r], scale=pv(A2, l, c, j), bias=modv(l, 3, c, j))
                for fc in range(22):
                    pg = psum(); pu = psum()
                    for kc in range(8):
                        mm(pg, pg.t[:, :N], Wg.t[:, kc, fc * 128:(fc + 1) * 128], h.t[:, kc, :], kc == 0, kc == 7, [Wg.r, h.r])
                    for kc in range(8):
                        mm(pu, pu.t[:, :N], Wu.t[:, kc, fc * 128:(fc + 1) * 128], h.t[:, kc, :], kc == 0, kc == 7, [Wu.r, h.r])
                    s_ = sg[fc % 2]
                    act(s_.t[:], pg.t[:, :N], AF.Silu, [pg.r], [s_.r])
                    tt("dve", hid.t[:, fc, :], s_.t[:], pu.t[:, :N], ALU.mult, [s_.r, pu.r], [hid.r])
                for dc in range(8):
                    ps = psum()
                    for kc in range(22):
                        mm(ps, ps.t[:, :N], Wd.t[:, kc, dc * 128:(dc + 1) * 128], hid.t[:, kc, :], kc == 0, kc == 21, [Wd.r, hid.r])
                    cp("act", mt.t[:, dc, :], ps.t[:, :N], [ps.r], [mt.r])
                tt("pool", sq.t[:], mt.t[:], mt.t[:], ALU.mult, [mt.r], [sq.r])
                ps = psum()
                for c in range(8):
                    mm(ps, ps.t[:, :N], ones_b, sq.t[:, c, :], c == 0, c == 7, [cb.r, sq.r])
                rstd_from_ps(ps, N, lnv, rstd, 1.0 / D)
                for c in range(8):
                    tm = tmp[c % 2]
                    tt("dve", tm.t[:], mt.t[:, c, :], rstd.t[:], ALU.mult, [mt.r, rstd.r], [tm.r])
                    stt("dve", x_.t[:, c, :], tm.t[:], pv(G2, l, c, j), x_.t[:, c, :], ALU.mult, ALU.add, [tm.r, G2.r, x_.r], [x_.r])
                if last and ti >= 1:
                    P.dma(out_d[:, t0 - TC:t0 - TC + N].rearrange("(c p) t -> p c t", p=128), x_.t[:], reads=[x_.r])
                else:
                    P.dma(res_d[:, t0:t0 + N].rearrange("(c p) t -> p c t", p=128), x_.t[:], reads=[x_.r, R_res], writes=[R_res])
            P.flush()
        ffn.close()
        P.barrier()
        if stop == ("D", l): return finish()

    return finish()


def _bf(a):
    return np.ascontiguousarray(a.astype(np.float32)).astype(ml_dtypes.bfloat16)

def _consts():
    ones = np.ones((128, 128), np.float32)
    ident = np.eye(128, dtype=np.float32)
    c = np.arange(128)
    same = (c[:, None] // 64) == (c[None, :] // 64)
    ang = 2 * np.pi * ((c[:, None] % 64) * (c[None, :] % 64)) / 64.0
    CDm = np.concatenate([np.where(same, np.cos(ang), 0.0), np.where(same, -np.sin(ang), 0.0)], axis=1)
    t1 = np.arange(64)
    a64 = 2 * np.pi * np.outer(t1, t1) / 64.0
    Cw, Sw = np.cos(a64), np.sin(a64)
    W1 = np.block([[Cw, -Sw], [Sw, Cw]])
    t2 = np.arange(64)[:, None, None]; f1 = np.arange(64)[None, :, None]; f2 = np.arange(64)[None, None, :]
    th = 2 * np.pi * (t2 * f1 / 4096.0 + t2 * f2 / 64.0)
    sc = 1.0 / np.sqrt(4096.0 * 64.0)
    M3 = np.concatenate([np.cos(th), np.sin(th)], axis=0) * sc
    t = np.arange(256)
    a256 = 2 * np.pi * np.outer(t, t) / 256.0
    sc2 = 1.0 / np.sqrt(256.0 * 64.0)
    C2 = (np.cos(a256) * sc2).reshape(2, 128, 256).transpose(1, 0, 2).reshape(128, 512)
    S2 = (np.sin(a256) * sc2).reshape(2, 128, 256).transpose(1, 0, 2).reshape(128, 512)
    cbf = np.concatenate([ones, ident, CDm, W1, C2, S2], axis=1)
    j = np.arange(128)[:, None]; i = np.arange(128)[None, :]
    cf = np.concatenate([(i >= j), (i <= j)], axis=1).astype(np.float32)
    return _bf(cbf), np.ascontiguousarray(cf), _bf(M3.reshape(128, 4096))

def _smalls(inp):
    sm = np.zeros((128, NS), np.float32)
    def put(name, arr):
        arr = np.asarray(arr, np.float32)
        sm[:, _off[name]:_off[name] + arr.shape[1]] = arr
    put("b_ada", np.asarray(inp["b_ada"]).reshape(L, 6, 8, 128).transpose(3, 0, 1, 2).reshape(128, L * 48))
    gains = np.stack([inp["g_pre_mix"], inp["g_post_mix"], inp["g_pre_ffn"], inp["g_post_ffn"]], axis=1)
    put("gain", gains.reshape(L, 4, 8, 128).transpose(3, 0, 1, 2).reshape(128, L * 32))
    put("g_gla", np.asarray(inp["g_gla"]).T)
    put("w_conv", np.asarray(inp["w_conv"]).reshape(L, 4, 2, 128).transpose(3, 0, 2, 1).reshape(128, L * 8))
    put("b_conv", np.asarray(inp["b_conv"]).reshape(L, 2, 128).transpose(2, 0, 1).reshape(128, L * 2))
    for nm in ("b_rg_a", "b_rg_x"):
        put(nm, np.asarray(inp[nm]).reshape(L, 2, 2, 128).transpose(3, 0, 1, 2).reshape(128, L * 4))
    put("lam", np.asarray(inp["rg_lam"]).reshape(L, 2, 2, 128).transpose(3, 0, 1, 2).reshape(128, L * 4))
    put("b_dec", np.asarray(inp["b_dec"]).reshape(L, 2, 2, 128).transpose(3, 0, 1, 2).reshape(128, L * 4))
    return sm

def _wrg(inp):
    w = np.zeros((L, 2, 2, 2, 128, 128), np.float32)
    for gi, nm in enumerate(("w_rg_a", "w_rg_x")):
        a = np.asarray(inp[nm], np.float32)
        for ch in range(2):
            for hh in range(2):
                w[:, :, gi, ch, hh * 64:(hh + 1) * 64, hh * 64:(hh + 1) * 64] = a[:, :, 2 * ch + hh]
    return w

def make_in_maps(inp):
    cbf, cf, m3 = _consts()
    sm = _smalls(inp)
    wrg = _wrg(inp)
    f = lambda k: np.ascontiguousarray(np.asarray(inp[k], np.float32))
    shared = {"smalls": sm, "w_ada": f("w_ada"), "w_in": f("w_in"), "w_out": f("w_out"), "w_ffn_gate": f("w_ffn_gate"),
              "w_ffn_up": f("w_ffn_up"), "w_ffn_down": f("w_ffn_down"), "w_dec": f("w_dec"), "wrg": wrg, "cbf": cbf, "cf32": cf, "m3": m3}
    x = np.asarray(inp["x"], np.float32); ctx = np.asarray(inp["ctx"], np.float32)
    c = np.asarray(inp["c"], np.float32); c_ctx = np.asarray(inp["c_ctx"], np.float32)
    maps = []
    for b in range(8):
        xT = np.ascontiguousarray(np.concatenate([ctx[b].T, x[b].T], axis=1))
        cc = np.stack([c[b], c_ctx], axis=1).reshape(8, 128, 2).transpose(1, 0, 2).reshape(128, 16)
        m = dict(shared); m["xT"] = xT; m["cc"] = np.ascontiguousarray(cc)
        maps.append(m)
    return maps

_NC = {}
def kernel(**inputs):
    if "nc" not in _NC:
        _NC["nc"] = build()
    maps = make_in_maps(inputs)
    res = run_bass_kernel_spmd(_NC["nc"], maps, core_ids=list(range(8)))
    out = np.stack([np.asarray(res.results[b]["outT"], np.float32).T for b in range(8)], axis=0)
    return np.ascontiguousarray(out)
```

```python
import contextlib
import numpy as np
import ml_dtypes
import concourse.bass as bass
import concourse.mybir as mybir
from concourse.bass_utils import run_bass_kernel_spmd

F32 = mybir.dt.float32
BF16 = mybir.dt.bfloat16
AF = mybir.ActivationFunctionType
ALU = mybir.AluOpType

D = 1024; TL = 4096; TC = 256; NT = TL + TC; L = 4
NIN = 2336; DFF = 2816; NCH = NT // 128
EPS = 1e-6
TT = [(0, 256)] + [(256 + 512 * i, 512) for i in range(8)]
TT256 = [(256 * i, 256) for i in range(17)]
FMCOLS = [0, 128, 256, 384, 1056, 1184, 1312, 1440, 1568, 1696, 1824, 1952, 2080, 2208]
ZQ, ZK, ZOG, ZF, ZRX, ZRG = 0, 2, 4, 8, 10, 12

_off = {}
_ns = 0
def _reg_small(name, n):
    global _ns
    _off[name] = _ns; _ns += n
_reg_small("b_ada", L * 48); _reg_small("gain", L * 32); _reg_small("g_gla", L)
_reg_small("w_conv", L * 8); _reg_small("b_conv", L * 2); _reg_small("b_rg_a", L * 4)
_reg_small("b_rg_x", L * 4); _reg_small("lam", L * 4); _reg_small("b_dec", L * 4)
NS = _ns


class Reg:
    __slots__ = ("w", "rl", "rd", "noread")
    def __init__(self):
        self.w = None; self.rl = {}; self.rd = []; self.noread = False

class Ins:
    __slots__ = ("eng", "fn", "deps", "need", "stream", "ticket", "is_dma")
    def __init__(self, eng, fn):
        self.eng = eng; self.fn = fn; self.deps = []; self.need = False
        self.stream = None; self.ticket = None; self.is_dma = False

ENGS = ("pe", "act", "dve", "pool", "sp")
NDMASEM = 8

class Prog:
    def __init__(self, nc, es):
        self.nc = nc
        self.q = {e: [] for e in ENGS}
        self.ndma = {e: 0 for e in ENGS}
        self.dma_last = {}
        self.cnt = {e: 0 for e in ENGS}
        self.seen = {e: {} for e in ENGS}
        self.last = {e: None for e in ENGS}
        self.regs = []
        self.bar = None
        self.sems = {}
        for e in ENGS:
            self.sems[e] = es.enter_context(nc.semaphore("s_" + e))
        for k in range(NDMASEM):
            self.sems[("sp", k)] = es.enter_context(nc.semaphore("d_sp%d" % k))

    def reg(self, persist=False):
        r = Reg()
        if persist: self.regs.append(r)
        return r

    def _track(self, ins, reads, writes, acc):
        deps = ins.deps
        for t in reads:
            if t.w is not None: deps.append(t.w)
        for t in writes:
            if t.w is not None: deps.append(t.w)
            deps.extend(t.rl.values()); deps.extend(t.rd)
        for t in reads:
            if t.noread: continue
            if ins.is_dma: t.rd.append(ins)
            else: t.rl[ins.eng] = ins
        for t in writes:
            t.w = ins; t.rl = {}; t.rd = []
        for t in acc:
            t.w = ins
        if self.bar is not None and self.bar[0].get(ins.eng):
            deps.extend(self.bar[1]); self.bar[0][ins.eng] = False
        for d in deps: d.need = True

    def op(self, eng, fn, reads=(), writes=(), acc=()):
        ins = Ins(eng, fn)
        self._track(ins, reads, writes, acc)
        self.q[eng].append(ins)
        return ins

    def dma(self, out, in_, reads=(), writes=(), eng="sp", **kw):
        ins = Ins(eng, lambda e: e.dma_start(out=out, in_=in_, **kw))
        ins.is_dma = True
        i = self.ndma[eng]; self.ndma[eng] += 1
        ins.stream = (eng, i % NDMASEM); ins.ticket = 16 * (i // NDMASEM + 1)
        prev = self.dma_last.get(ins.stream)
        if prev is not None: ins.deps.append(prev)
        self.dma_last[ins.stream] = ins
        ins.need = True
        self._track(ins, reads, writes, ())
        self.q[eng].append(ins)
        return ins

    def barrier(self):
        pend = [x for x in self.last.values() if x is not None] + list(self.dma_last.values())
        for e in ENGS:
            if self.q[e]: pend.append(self.q[e][-1])
        for x in pend: x.need = True
        self.bar = ({e: True for e in ENGS}, pend)

    def flush(self, final=False):
        nc = self.nc
        for t in self.regs:
            if t.w is not None: t.w.need = True
            for r in t.rl.values(): r.need = True
            for r in t.rd: r.need = True
        for e in ENGS:
            if self.q[e]: self.q[e][-1].need = True
        for e in ENGS:
            for ins in self.q[e]:
                if ins.is_dma or ins.ticket is not None: continue
                ins.stream = e
                if ins.need:
                    self.cnt[e] += 1; ins.ticket = self.cnt[e]
        sems = self.sems
        def run(e, eng):
            seen = self.seen[e]
            for ins in self.q[e]:
                need = {}
                for d in ins.deps:
                    if d.ticket is None:
                        raise RuntimeError("dep without ticket")
                    if d.ticket > need.get(d.stream, 0): need[d.stream] = d.ticket
                for s, v in need.items():
                    if seen.get(s, 0) >= v: continue
                    eng.wait_ge(sems[s], v); seen[s] = v
                r = ins.fn(eng)
                if ins.need:
                    r.then_inc(sems[ins.stream], 16 if ins.is_dma else 1)
            if final:
                for s, lastd in self.dma_last.items():
                    if s[0] == e and seen.get(s, 0) < lastd.ticket:
                        eng.wait_ge(sems[s], lastd.ticket); seen[s] = lastd.ticket
        with nc.Block() as block:
            if self.q["sp"]:
                @block.sync
                def _(eng): run("sp", eng)
            if self.q["pe"]:
                @block.tensor
                def _(eng): run("pe", eng)
            if self.q["act"]:
                @block.scalar
                def _(eng): run("act", eng)
            if self.q["dve"]:
                @block.vector
                def _(eng): run("dve", eng)
            if self.q["pool"]:
                @block.gpsimd
                def _(eng): run("pool", eng)
        for e in ENGS:
            if self.q[e]: self.last[e] = self.q[e][-1]
            self.q[e] = []


class Tile:
    def __init__(self, t, r):
        self.t = t; self.r = r


def build(n_layers=L, dump=(), stop=None):
    nc = bass.Bass("TRN2", target_bir_lowering=False)
    top = contextlib.ExitStack()
    P = Prog(nc, top)

    def din(name, shape, dt=F32):
        return nc.dram_tensor(name, shape, dt, kind="ExternalInput").ap()
    def dscr(name, shape, dt):
        kind = "ExternalOutput" if name in dump else "Internal"
        return nc.dram_tensor(name, shape, dt, kind=kind).ap()

    xT_d = din("xT", [D, NT]); cc_d = din("cc", [128, 16]); sm_d = din("smalls", [128, NS])
    w_ada_d = din("w_ada", [L, D, 6 * D]); w_in_d = din("w_in", [L, D, NIN]); w_out_d = din("w_out", [L, D, D])
    w_g_d = din("w_ffn_gate", [L, D, DFF]); w_u_d = din("w_ffn_up", [L, D, DFF]); w_d_d = din("w_ffn_down", [L, DFF, D])
    w_dec_d = din("w_dec", [L, 2, 16, 256]); wrg_d = din("wrg", [L, 2, 2, 2, 128, 128])
    NCB = 128 * 2 + 256 + 128 + 4 * 256
    cb_d = din("cbf", [128, NCB], BF16)
    m3_d = din("m3", [128, 4096], BF16)
    cf_d = din("cf32", [128, 256])
    out_d = nc.dram_tensor("outT", [D, TL], F32, kind="ExternalOutput").ap()

    res_d = dscr("res", [D, NT], F32)
    zT_d = dscr("zT", [14 * 128, NT], BF16)
    dec_d = dscr("decT", [32, NT], F32)
    vtok_d = dscr("vtok", [NT, 512], BF16)
    yT_d = dscr("yT", [D, NT], BF16)
    Yd_d = dscr("Yd", [TL, 512], BF16)
    Ad_d = dscr("Ad", [128, 64 * 256], BF16)
    R_res, R_z, R_dec, R_v, R_y, R_Yd, R_Ad = (P.reg(True) for _ in range(7))

    uid = {"i": 0}
    def SB(es, name, shape, dt=F32):
        uid["i"] += 1
        return es.enter_context(nc.sbuf_tensor("%s_%d" % (name, uid["i"]), shape, dt))
    def mk(es, name, shape, dt=F32, persist=False):
        return Tile(SB(es, name, shape, dt), P.reg(persist))

    psb = [Tile(top.enter_context(nc.psum_tensor("ps%d" % i, [128, 512], F32)), P.reg(True)) for i in range(6)]
    psM = Tile(top.enter_context(nc.psum_tensor("psM", [128, 512], F32)), P.reg(True))
    psT = Tile(top.enter_context(nc.psum_tensor("psT", [128, 1024], BF16)), P.reg(True))
    pstate = {"i": 0}
    def psum():
        p = psb[pstate["i"] % 6]; pstate["i"] += 1; return p

    cb = mk(top, "cb", [128, NCB], BF16, True)
    cf = mk(top, "cf", [128, 256], F32, True)
    sm = mk(top, "sm", [128, NS], F32, True)
    P.dma(cb.t[:], cb_d, writes=[cb.r]); P.dma(cf.t[:], cf_d, writes=[cf.r]); P.dma(sm.t[:], sm_d, writes=[sm.r])
    ones_b = cb.t[:, 0:128]; ident_b = cb.t[:, 128:256]; CD = cb.t[:, 256:512]; W1 = cb.t[:, 512:640]
    C256 = cb.t[:, 640:640 + 512].rearrange("p (a b) -> p a b", a=2)
    S256 = cb.t[:, 1152:1152 + 512].rearrange("p (a b) -> p a b", a=2)
    maskFB = cf.t[:, 0:256]
    def smv(name, idx, n=1):
        o = _off[name] + idx
        return sm.t[:, o:o + n]

    mod = mk(top, "mod", [128, L * 96], F32, True)
    A1 = mk(top, "A1", [128, L * 16], F32, True); G1 = mk(top, "G1", [128, L * 16], F32, True)
    A2 = mk(top, "A2", [128, L * 16], F32, True); G2 = mk(top, "G2", [128, L * 16], F32, True)
    cl = mk(top, "cl", [128, L * 4], F32, True); cl2 = mk(top, "cl2", [128, L * 4], F32, True); nbdec = mk(top, "nbdec", [128, L * 4], F32, True)
    def modv(l, g, fc, j):
        o = l * 96 + g * 16 + fc * 2 + j
        return mod.t[:, o:o + 1]
    def pv(tile_, l, fc, j):
        o = l * 16 + fc * 2 + j
        return tile_.t[:, o:o + 1]

    def act(out, in_, func, reads, writes, scale=1.0, bias=0.0, eng="act"):
        return P.op("act", lambda e: e.activation(out=out, in_=in_, func=func, scale=scale, bias=bias), reads=reads, writes=writes)
    def tt(eng, out, a, b, op, reads, writes):
        return P.op(eng, lambda e: e.tensor_tensor(out=out, in0=a, in1=b, op=op), reads=reads, writes=writes)
    def ts(eng, out, a, s1, s2, op0, op1, reads, writes):
        if s2 is None:
            return P.op(eng, lambda e: e.tensor_scalar(out=out, in0=a, scalar1=s1, scalar2=None, op0=op0), reads=reads, writes=writes)
        return P.op(eng, lambda e: e.tensor_scalar(out=out, in0=a, scalar1=s1, scalar2=s2, op0=op0, op1=op1), reads=reads, writes=writes)
    def stt(eng, out, a, s, b, op0, op1, reads, writes):
        return P.op(eng, lambda e: e.scalar_tensor_tensor(out=out, in0=a, scalar=s, in1=b, op0=op0, op1=op1), reads=reads, writes=writes)
    def cp(eng, out, in_, reads, writes):
        if eng == "act":
            return P.op("act", lambda e: e.activation(out=out, in_=in_, func=AF.Copy), reads=reads, writes=writes)
        return P.op(eng, lambda e: e.tensor_copy(out=out, in_=in_), reads=reads, writes=writes)
    def mmx(ps, out, lhsT, rhs, start, stop, reads, newtile):
        if newtile:
            return P.op("pe", lambda e: e.matmul(out, lhsT=lhsT, rhs=rhs, start=start, stop=stop), reads=reads, writes=[ps.r])
        return P.op("pe", lambda e: e.matmul(out, lhsT=lhsT, rhs=rhs, start=start, stop=stop), reads=reads, acc=[ps.r])
    def mm(ps, out, lhsT, rhs, first, last, reads):
        return mmx(ps, out, lhsT, rhs, first, last, reads, first)

    evs = {"i": 0}
    def evac(out, in_, reads, writes):
        evs["i"] += 1
        return cp("act" if evs["i"] % 2 else "dve", out, in_, reads, writes)

    def rstd_from_ps(ps, N, lnv, rstd, inv_n):
        act(lnv.t[:, :N], ps.t[:, :N], AF.Ln, [ps.r], [lnv.r], scale=inv_n, bias=EPS)
        act(rstd.t[:, :N], lnv.t[:, :N], AF.Exp, [lnv.r], [rstd.r], scale=-0.5)

    class WLoader:
        def __init__(self, es, ncols=1024, engs=("act", "dve"), nbuf=2):
            self.st = [mk(es, "wst%d" % i, [128, ncols]) for i in range(nbuf)]
            self.i = 0; self.nc = ncols; self.engs = engs; self.nb = nbuf
        def piece(self, dst, dreg, src, n, eng=None):
            st = self.st[self.i % self.nb]
            e = eng or self.engs[self.i % len(self.engs)]
            self.i += 1
            P.dma(st.t[:, :n], src, writes=[st.r])
            cp(e, dst, st.t[:, :n], [st.r], [dreg])
        def load(self, dst, dreg, src, ncols, eng=None):
            for c0 in range(0, ncols, self.nc):
                n = min(self.nc, ncols - c0)
                self.piece(dst[:, c0:c0 + n], dreg, src[:, c0:c0 + n], n, eng)

    modst = {}
    def emit_mod_groups(l, wst, groups):
        ps = psM
        for g in groups:
            w = wst[modst.setdefault("k", 0) % 2]; modst["k"] += 1
            P.dma(w.t[:], w_ada_d[l, :, g * 1024:(g + 1) * 1024].rearrange("(kc p) n -> p kc n", p=128), writes=[w.r])
            for fc in range(8):
                o = (g * 8 + fc) * 2
                for kc in range(8):
                    mmx(ps, ps.t[:, o:o + 2], w.t[:, kc, fc * 128:(fc + 1) * 128], modst["sv"].t[:, kc * 2:kc * 2 + 2],
                        kc == 0, kc == 7, [w.r, modst["sv"].r], (g == 0 and fc == 0 and kc == 0))
    def emit_mod_finish(l):
        ps = psM
        mv = mod.t[:, l * 96:(l + 1) * 96].rearrange("p (a j) -> p a j", j=2)
        pv2 = ps.t[:, 0:96].rearrange("p (a j) -> p a j", j=2)
        for j in range(2):
            tt("dve", mv[:, :, j], pv2[:, :, j], smv("b_ada", l * 48, 48), ALU.add, [ps.r, sm.r], [mod.r])
        for (dst, gsc, gidx) in ((A1, 1, 0), (A2, 4, 2)):
            dv = dst.t[:, l * 16:(l + 1) * 16].rearrange("p (a j) -> p a j", j=2)
            scv = mod.t[:, l * 96 + gsc * 16: l * 96 + gsc * 16 + 16].rearrange("p (a j) -> p a j", j=2)
            for j in range(2):
                stt("dve", dv[:, :, j], scv[:, :, j], 1.0, smv("gain", l * 32 + gidx * 8, 8), ALU.add, ALU.mult, [mod.r, sm.r], [dst.r])
        for (dst, ggt, gidx) in ((G1, 2, 1), (G2, 5, 3)):
            dv = dst.t[:, l * 16:(l + 1) * 16].rearrange("p (a j) -> p a j", j=2)
            gv = mod.t[:, l * 96 + ggt * 16: l * 96 + ggt * 16 + 16].rearrange("p (a j) -> p a j", j=2)
            for j in range(2):
                tt("dve", dv[:, :, j], gv[:, :, j], smv("gain", l * 32 + gidx * 8, 8), ALU.mult, [mod.r, sm.r], [dst.r])

    sv = mk(top, "sv", [128, 16], F32, True)
    modst["sv"] = sv
    with contextlib.ExitStack() as ph:
        cc = mk(ph, "cc_sb", [128, 16])
        P.dma(cc.t[:], cc_d, writes=[cc.r])
        act(sv.t[:], cc.t[:], AF.Silu, [cc.r], [sv.r])
        wst0 = [mk(ph, "wada%d" % i, [128, 8, 1024]) for i in range(2)]
        emit_mod_groups(0, wst0, range(6))
        emit_mod_finish(0)
        tmp = mk(ph, "tmpl", [128, L * 4])
        act(tmp.t[:], smv("lam", 0, L * 4), AF.Exp, [sm.r], [tmp.r], scale=-1.0)
        act(tmp.t[:], tmp.t[:], AF.Ln, [tmp.r], [tmp.r], scale=1.0, bias=1.0)
        ts("dve", cl.t[:], tmp.t[:], -8.0, None, ALU.mult, None, [tmp.r], [cl.r])
        ts("dve", cl2.t[:], tmp.t[:], -16.0, None, ALU.mult, None, [tmp.r], [cl2.r])
        ts("dve", nbdec.t[:], smv("b_dec", 0, L * 4), -1.0, None, ALU.mult, None, [sm.r], [nbdec.r])
        P.flush()
    P.barrier()
    for t_ in (cb, cf, sm, mod, A1, G1, A2, G2, cl, cl2, nbdec, sv):
        t_.r.noread = True

    def finish():
        P.flush(final=True)
        top.close()
        return nc

    for l in range(n_layers):
        src_d = xT_d if l == 0 else res_d
        src_reads = [] if l == 0 else [R_res]
        with contextlib.ExitStack() as ph:
            Win = mk(ph, "Win", [128, 8, NIN], BF16)
            wl = WLoader(ph, 1168, ("act", "dve"))
            for kc in range(8):
                wl.load(Win.t[:, kc, :], Win.r, w_in_d[l, kc * 128:(kc + 1) * 128, :], NIN)
            xs = [mk(ph, "xs%d" % i, [128, 8, 512]) for i in range(2)]
            sq = mk(ph, "sq", [128, 8, 512], BF16)
            h = mk(ph, "h", [128, 8, 512], BF16)
            lnv = mk(ph, "lnv", [128, 512]); rstd = mk(ph, "rstd", [128, 512])
            tmp = [mk(ph, "tmp%d" % i, [128, 512]) for i in range(2)]
            zst = [mk(ph, "zst%d" % i, [128, 14, 512], BF16) for i in range(2)]
            vst = mk(ph, "vst", [128, 4, 512], BF16)
            dst_ = mk(ph, "dst", [32, 512])
            def load_x(ti):
                t0, N = TT[ti]
                x_ = xs[ti % 2]
                P.dma(x_.t[:, :, :N], src_d[:, t0:t0 + N].rearrange("(c p) t -> p c t", p=128), reads=src_reads, writes=[x_.r])
            load_x(0)
            for ti, (t0, N) in enumerate(TT):
                j = 1 if ti == 0 else 0
                if ti + 1 < len(TT): load_x(ti + 1)
                x_ = xs[ti % 2]
                tt("pool", sq.t[:, :, :N], x_.t[:, :, :N], x_.t[:, :, :N], ALU.mult, [x_.r], [sq.r])
                ps = psum()
                for c in range(8):
                    mm(ps, ps.t[:, :N], ones_b, sq.t[:, c, :N], c == 0, c == 7, [cb.r, sq.r])
                rstd_from_ps(ps, N, lnv, rstd, 1.0 / D)
                for c in range(8):
                    tm = tmp[c % 2]
                    tt("dve", tm.t[:, :N], x_.t[:, c, :N], rstd.t[:, :N], ALU.mult, [x_.r, rstd.r], [tm.r])
                    act(h.t[:, c, :N], tm.t[:, :N], AF.Identity, [tm.r, A1.r, mod.r], [h.r], scale=pv(A1, l, c, j), bias=modv(l, 0, c, j))
                zs = zst[ti % 2]
                for ci, col in enumerate(FMCOLS):
                    ps = psum()
                    for kc in range(8):
                        mm(ps, ps.t[:, :N], Win.t[:, kc, col:col + 128], h.t[:, kc, :N], kc == 0, kc == 7, [Win.r, h.r])
                    evac(zs.t[:, ci, :N], ps.t[:, :N], [ps.r], [zs.r])
                P.dma(zT_d[:, t0:t0 + N].rearrange("(c p) t -> p c t", p=128), zs.t[:, :, :N], reads=[zs.r], writes=[R_z])
                ps = psum()
                for kc in range(8):
                    mm(ps, ps.t[0:32, :N], Win.t[:, kc, 1024:1056], h.t[:, kc, :N], kc == 0, kc == 7, [Win.r, h.r])
                evac(dst_.t[:, :N], ps.t[0:32, :N], [ps.r], [dst_.r])
                P.dma(dec_d[:, t0:t0 + N], dst_.t[:, :N], reads=[dst_.r], writes=[R_dec])
                ns = N // 128
                for s in range(ns):
                    ps = psum()
                    for kc in range(8):
                        mm(ps, ps.t[:, :512], h.t[:, kc, s * 128:(s + 1) * 128], Win.t[:, kc, 512:1024], kc == 0, kc == 7, [Win.r, h.r])
                    evac(vst.t[:, s, :], ps.t[:, :512], [ps.r], [vst.r])
                P.dma(vtok_d[t0:t0 + N, :].rearrange("(s p) v -> p s v", p=128), vst.t[:, :ns, :], reads=[vst.r], writes=[R_v])
            P.flush()
        P.barrier()
        if stop == ("A", l): return finish()

        with contextlib.ExitStack() as ph:
            S = [mk(ph, "S%d" % i, [128, NT]) for i in range(6)]
            H = [mk(ph, "H%d" % i, [128, NT], BF16) for i in range(3)]
            wst = mk(ph, "wrgst", [128, 4, 128]); wb = mk(ph, "wrgb", [128, 4, 128], BF16)
            nxt_mod = (l + 1 < n_layers)
            if nxt_mod:
                wstm = [mk(ph, "wadaB%d" % i, [128, 8, 1024]) for i in range(2)]
            for ch in range(2):
                P.dma(H[0].t[:], zT_d[(ZRX + ch) * 128:(ZRX + ch + 1) * 128, :], reads=[R_z], writes=[H[0].r])
                P.dma(wst.t[:], wrg_d[l, :, :, ch].rearrange("d g k m -> k (d g) m"), writes=[wst.r])
                cp("pool", wb.t[:], wst.t[:], [wst.r], [wb.r])
                z = H[0]; u = S[0]
                wc = lambda tap: smv("w_conv", l * 8 + ch * 4 + tap)
                ts("dve", u.t[:], z.t[:], wc(1), smv("b_conv", l * 2 + ch), ALU.mult, ALU.add, [z.r, sm.r], [u.r])
                for (lo, n, rows, w) in ((0, TC, 1, TC), (TC, TL, 64, 64)):
                    zv = z.t[:, lo:lo + n].rearrange("p (r w) -> p r w", w=w)
                    uv = u.t[:, lo:lo + n].rearrange("p (r w) -> p r w", w=w)
                    stt("dve", uv[:, :, 1:w], zv[:, :, 0:w - 1], wc(0), uv[:, :, 1:w], ALU.mult, ALU.add, [z.r, sm.r, u.r], [u.r])
                    stt("dve", uv[:, :, 0:w - 1], zv[:, :, 1:w], wc(2), uv[:, :, 0:w - 1], ALU.mult, ALU.add, [z.r, sm.r, u.r], [u.r])
                    stt("dve", uv[:, :, 0:w - 2], zv[:, :, 2:w], wc(3), uv[:, :, 0:w - 2], ALU.mult, ALU.add, [z.r, sm.r, u.r], [u.r])
                ub = H[1]
                cp("act", ub.t[:], u.t[:], [u.r], [ub.r])
                for d in range(2):
                    r_, i_, a_ = S[1], S[2], S[3]
                    hd = S[4 + d]
                    for (gate, dstt, bname) in ((0, r_, "b_rg_a"), (1, i_, "b_rg_x")):
                        for (t0, N) in TT:
                            ps = psum()
                            mm(ps, ps.t[:, :N], wb.t[:, d * 2 + gate, :], ub.t[:, t0:t0 + N], True, True, [wb.r, ub.r])
                            act(dstt.t[:, t0:t0 + N], ps.t[:, :N], AF.Sigmoid, [ps.r, sm.r], [dstt.r], bias=smv(bname, l * 4 + d * 2 + ch))
                    o4 = l * 4 + d * 2 + ch
                    act(a_.t[:], r_.t[:], AF.Exp, [r_.r, cl.r], [a_.r], scale=cl.t[:, o4:o4 + 1])
                    act(r_.t[:], r_.t[:], AF.Exp, [r_.r, cl2.r], [r_.r], scale=cl2.t[:, o4:o4 + 1])
                    ts("dve", r_.t[:], r_.t[:], -1.0, 1.0, ALU.mult, ALU.add, [r_.r], [r_.r])
                    act(r_.t[:], r_.t[:], AF.Sqrt, [r_.r], [r_.r])
                    tt("pool", i_.t[:], i_.t[:], u.t[:], ALU.mult, [i_.r, u.r], [i_.r])
                    tt("dve", r_.t[:], r_.t[:], i_.t[:], ALU.mult, [r_.r, i_.r], [r_.r])
                    if d == 0:
                        P.op("dve", lambda e, hd=hd, a_=a_, r_=r_: e.tensor_tensor_scan(out=hd.t[:, 0:TC], data0=a_.t[:, 0:TC], data1=r_.t[:, 0:TC], initial=0.0, op0=ALU.mult, op1=ALU.add),
                             reads=[a_.r, r_.r], writes=[hd.r])
                        P.op("dve", lambda e, hd=hd, a_=a_, r_=r_: e.tensor_tensor_scan(out=hd.t[:, TC:NT], data0=a_.t[:, TC:NT], data1=r_.t[:, TC:NT], initial=hd.t[:, TC - 1:TC], op0=ALU.mult, op1=ALU.add),
                             reads=[a_.r, r_.r, hd.r], writes=[hd.r])
                    else:
                        P.op("dve", lambda e, hd=hd, a_=a_, r_=r_: e.tensor_tensor_scan(out=hd.t[:, 0:TC][:, ::-1], data0=a_.t[:, 0:TC][:, ::-1], data1=r_.t[:, 0:TC][:, ::-1], initial=0.0, op0=ALU.mult, op1=ALU.add),
                             reads=[a_.r, r_.r], writes=[hd.r])
                        P.op("dve", lambda e, hd=hd, a_=a_, r_=r_: e.tensor_tensor_scan(out=hd.t[:, TC:NT][:, ::-1], data0=a_.t[:, TC:NT][:, ::-1], data1=r_.t[:, TC:NT][:, ::-1], initial=hd.t[:, 0:1], op0=ALU.mult, op1=ALU.add),
                             reads=[a_.r, r_.r, hd.r], writes=[hd.r])
                tt("pool", S[4].t[:], S[4].t[:], S[5].t[:], ALU.add, [S[4].r, S[5].r], [S[4].r])
                P.dma(H[0].t[:], zT_d[(ZRG + ch) * 128:(ZRG + ch + 1) * 128, :], reads=[R_z], writes=[H[0].r])
                act(S[5].t[:], H[0].t[:], AF.Gelu, [H[0].r], [S[5].r])
                tt("dve", H[2].t[:], S[4].t[:], S[5].t[:], ALU.mult, [S[4].r, S[5].r], [H[2].r])
                if nxt_mod and ch == 1:
                    emit_mod_groups(l + 1, wstm, range(6))
                    emit_mod_finish(l + 1)
                P.dma(yT_d[768 + ch * 128: 768 + (ch + 1) * 128, :], H[2].t[:], reads=[H[2].r], writes=[R_y])
            P.flush()
        P.barrier()
        if stop == ("B1", l): return finish()

        with contextlib.ExitStack() as ph:
            mF = mk(ph, "mF", [128, NT + 1], BF16)
            P.op("pool", lambda e: e.memset(mF.t[:], 1.0), writes=[mF.r])
            P.op("pool", lambda e: e.memset(mF.t[:, 0:NT + 1:128], 0.0), writes=[mF.r])
            lrt = mk(ph, "lr", [48, NT])
            wdt = mk(ph, "wdec", [48, 256])
            for d in range(2):
                P.dma(lrt.t[d * 32:d * 32 + 16, :], dec_d[d * 16:(d + 1) * 16, :], reads=[R_dec], writes=[lrt.r])
                P.dma(wdt.t[d * 32:d * 32 + 16, :], w_dec_d[l, d], writes=[wdt.r])
            vtok = mk(ph, "vtok", [128, NCH, 256], BF16)
            Sa = mk(ph, "Sa", [128, NT])
            Hq = mk(ph, "Hq", [128, NT], BF16); Hk = mk(ph, "Hk", [128, NT], BF16)
            qt = [mk(ph, "qt%d" % d, [128, NT], BF16) for d in range(2)]
            kt = [mk(ph, "kt%d" % d, [128, NT], BF16) for d in range(2)]
            ktok = [mk(ph, "ktok%d" % d, [128, NCH, 128], BF16) for d in range(2)]
            ebc = [mk(ph, "ebc%d" % d, [128, NCH]) for d in range(2)]
            Sbf = [mk(ph, "Sbf%d" % d, [128, NCH, 256], BF16) for d in range(2)]
            Scur = [[mk(ph, "Sc%d%d" % (d, i), [128, 256]) for i in range(2)] for d in range(2)]
            kvs = [mk(ph, "kvs%d" % i, [128, 256]) for i in range(4)]
            am = [mk(ph, "am%d" % i, [128, 256], BF16) for i in range(4)]
            Hog = Hq; yg = Hq
            sqo = mk(ph, "sqo", [128, 512], BF16); lnv = mk(ph, "lnv2", [128, 512]); rstd = mk(ph, "rstd2", [128, 512])
            t1 = mk(ph, "t1", [128, 512]); t2 = mk(ph, "t2", [128, 512])
            order = [list(range(NCH)), [1, 0] + list(range(NCH - 1, 1, -1))]
            for hp in range(2):
                P.dma(Hq.t[:], zT_d[(ZQ + hp) * 128:(ZQ + hp + 1) * 128, :], reads=[R_z], writes=[Hq.r])
                P.dma(Hk.t[:], zT_d[(ZK + hp) * 128:(ZK + hp + 1) * 128, :], reads=[R_z], writes=[Hk.r])
                for q4 in range(2):
                    P.dma(vtok.t[:, q4 * 17:(q4 + 1) * 17, :], vtok_d[q4 * 17 * 128:(q4 + 1) * 17 * 128, hp * 256:(hp + 1) * 256].rearrange("(s p) v -> p s v", p=128), reads=[R_v], writes=[vtok.r])
                for d in range(2):
                    o4 = l * 4 + d * 2 + hp
                    for (t0, N) in TT:
                        ps = psum()
                        mm(ps, ps.t[:, :N], wdt.t[d * 32:d * 32 + 16, hp * 128:(hp + 1) * 128], lrt.t[d * 32:d * 32 + 16, t0:t0 + N], True, True, [wdt.r, lrt.r])
                        act(Sa.t[:, t0:t0 + N], ps.t[:, :N], AF.Exp, [ps.r, nbdec.r], [Sa.r], scale=-1.0, bias=nbdec.t[:, o4:o4 + 1])
                    act(Sa.t[:], Sa.t[:], AF.Ln, [Sa.r], [Sa.r], scale=1.0, bias=1.0)
                    if d == 0:
                        P.op("dve", lambda e: e.tensor_tensor_scan(out=Sa.t[:], data0=mF.t[:, 0:NT], data1=Sa.t[:], initial=0.0, op0=ALU.mult, op1=ALU.add),
                             reads=[mF.r, Sa.r], writes=[Sa.r])
                    else:
                        P.op("dve", lambda e: e.tensor_tensor_scan(out=Sa.t[:, ::-1], data0=mF.t[:, 1:NT + 1][:, ::-1], data1=Sa.t[:, ::-1], initial=0.0, op0=ALU.mult, op1=ALU.add),
                             reads=[mF.r, Sa.r], writes=[Sa.r])
                    for ei, (t0, N) in enumerate(TT):
                        te = (t1, t2)[ei % 2]
                        act(te.t[:, :N], Sa.t[:, t0:t0 + N], AF.Exp, [Sa.r], [te.r], scale=1.0 / 16.0)
                        tt("dve", kt[d].t[:, t0:t0 + N], Hk.t[:, t0:t0 + N], te.t[:, :N], ALU.mult, [Hk.r, te.r], [kt[d].r])
                    act(Sa.t[:], Sa.t[:], AF.Exp, [Sa.r], [Sa.r], scale=-1.0 / 16.0)
                    stt("dve", qt[d].t[:], Hq.t[:], 0.125, Sa.t[:], ALU.mult, ALU.mult, [Hq.r, Sa.r], [qt[d].r])
                    e_at = 127 if d == 0 else 0
                    cp("pool", ebc[d].t[:], Sa.t[:, e_at:NT:128], [Sa.r], [ebc[d].r])
                    for n0 in range(0, NCH, 4):
                        nn = min(4, NCH - n0)
                        for i in range(nn):
                            n = n0 + i
                            P.op("pe", lambda e, d=d, n=n, i=i: e.transpose(psT.t[:, i * 128:(i + 1) * 128], kt[d].t[:, n * 128:(n + 1) * 128], ident_b),
                                 reads=[kt[d].r, cb.r], **({"writes": [psT.r]} if i == 0 else {"acc": [psT.r]}))
                        evac(ktok[d].t[:, n0:n0 + nn, :], psT.t[:, :nn * 128].rearrange("p (a b) -> p a b", b=128), [psT.r], [ktok[d].r])
                cur = [0, 0]
                for d in range(2):
                    P.op("pool", lambda e, d=d: e.memset(Scur[d][0].t[:], 0.0), writes=[Scur[d][0].r])
                    P.op("pool", lambda e, d=d: e.memset(Sbf[d].t[:, order[d][0], :], 0.0), writes=[Sbf[d].r])
                kvi = 0
                for s in range(NCH - 1):
                    for d in range(2):
                        n = order[d][s]; nxt = order[d][s + 1]
                        ps = psum()
                        mm(ps, ps.t[:, :256], ktok[d].t[:, n, :], vtok.t[:, n, :], True, True, [ktok[d].r, vtok.r])
                        kv = kvs[kvi % 4]; kvi += 1
                        act(kv.t[:], ps.t[:, :256], AF.Identity, [ps.r, ebc[d].r], [kv.r], scale=ebc[d].t[:, n:n + 1])
                        sc_, sn_ = Scur[d][cur[d] % 2], Scur[d][(cur[d] + 1) % 2]; cur[d] += 1
                        stt("dve", sn_.t[:], sc_.t[:], ebc[d].t[:, n:n + 1], kv.t[:], ALU.mult, ALU.add, [sc_.r, ebc[d].r, kv.r], [sn_.r])
                        cp("pool" if d == 0 else "dve", Sbf[d].t[:, nxt, :], sn_.t[:], [sn_.r], [Sbf[d].r])
                for hh in range(2):
                    head = 2 * hp + hh
                    Rw = slice(hh * 64, (hh + 1) * 64)
                    P.dma(Hog.t[:], zT_d[(ZOG + head) * 128:(ZOG + head + 1) * 128, :], reads=[R_z], writes=[Hog.r])
                    units = [(ti, ci) for ti, (t0, N) in enumerate(TT) for ci in range(N // 128)]
                    po_of = {}
                    st8 = {"ai": 0}
                    def S12(u, Rw=Rw):
                        ti, ci = u
                        n = TT[ti][0] // 128 + ci
                        cs = slice(n * 128, (n + 1) * 128)
                        pa = psb[2 + st8["ai"] % 3]
                        for d in range(2):
                            mmx(pa, pa.t[:, d * 128:(d + 1) * 128], kt[d].t[Rw, cs], qt[d].t[Rw, cs], True, True, [kt[d].r, qt[d].r], d == 0)
                        a_ = am[st8["ai"] % 4]; st8["ai"] += 1
                        tt("dve", a_.t[:], pa.t[:, :256], maskFB, ALU.mult, [pa.r, cf.r], [a_.r])
                        return a_
                    def S3(u, a_, Rw=Rw, hh=hh):
                        ti, ci = u
                        t0, N = TT[ti]
                        n = t0 // 128 + ci
                        cs = slice(n * 128, (n + 1) * 128)
                        if ci == 0: po_of[ti] = psb[ti % 2]
                        po = po_of[ti]
                        oc = po.t[:, ci * 128:(ci + 1) * 128]
                        vv = vtok.t[:, n, hh * 128:(hh + 1) * 128]
                        mmx(po, oc, vv, a_.t[:, 0:128], True, False, [vtok.r, a_.r], ci == 0)
                        mmx(po, oc, vv, a_.t[:, 128:256], False, False, [vtok.r, a_.r], False)
                        mmx(po, oc, Sbf[0].t[Rw, n, hh * 128:(hh + 1) * 128], qt[0].t[Rw, cs], False, False, [Sbf[0].r, qt[0].r], False)
                        mmx(po, oc, Sbf[1].t[Rw, n, hh * 128:(hh + 1) * 128], qt[1].t[Rw, cs], False, True, [Sbf[1].r, qt[1].r], False)
                        return ci == N // 128 - 1
                    def NORM(ti):
                        t0, N = TT[ti]
                        po = po_of[ti]
                        act(sqo.t[:, :N], po.t[:, :N], AF.Square, [po.r], [sqo.r])
                        pss = psb[5]
                        mm(pss, pss.t[:, :N], ones_b, sqo.t[:, :N], True, True, [cb.r, sqo.r])
                        rstd_from_ps(pss, N, lnv, rstd, 1.0 / 128.0)
                        tt("dve", t1.t[:, :N], po.t[:, :N], rstd.t[:, :N], ALU.mult, [po.r, rstd.r], [t1.r])
                        act(t2.t[:, :N], Hog.t[:, t0:t0 + N], AF.Silu, [Hog.r], [t2.r])
                        stt("dve", yg.t[:, t0:t0 + N], t1.t[:, :N], smv("g_gla", l), t2.t[:, :N], ALU.mult, ALU.mult, [t1.r, t2.r, sm.r], [yg.r])
                    pq = []; nq = []
                    def step3():
                        u0, a0 = pq.pop(0)
                        for e_ in nq: e_[1] -= 1
                        while nq and nq[0][1] <= 0:
                            NORM(nq.pop(0)[0])
                        if S3(u0, a0): nq.append([u0[0], 2])
                    for u in units:
                        pq.append((u, S12(u)))
                        if len(pq) > 2: step3()
                    while pq: step3()
                    while nq: NORM(nq.pop(0)[0])
                    P.dma(yT_d[head * 128:(head + 1) * 128, :], yg.t[:], reads=[yg.r], writes=[R_y])
            P.flush()
        P.barrier()
        if stop == ("B2", l): return finish()

        with contextlib.ExitStack() as ph:
            fT = [mk(ph, "fT%d" % kc, [128, NT], BF16) for kc in range(2)]
            for kc in range(2):
                P.dma(fT[kc].t[:], zT_d[(ZF + kc) * 128:(ZF + kc + 1) * 128, :], reads=[R_z], writes=[fT[kc].r])
            Yc = mk(ph, "Yc", [128, 2, 512], BF16)
            Yl = mk(ph, "Yl", [128, 32, 512], BF16)
            for s in range(NCH):
                ps = psum()
                for kc in range(2):
                    mmx(ps, ps.t[:, kc * 256:(kc + 1) * 256], fT[kc].t[:, s * 128:(s + 1) * 128], CD, True, True, [fT[kc].r, cb.r], kc == 0)
                dst = Yc if s < 2 else Yl
                si = s if s < 2 else s - 2
                evac(dst.t[:, si, :].rearrange("p (ri kc c) -> p kc ri c", ri=2, kc=2),
                     ps.t[:, :512].rearrange("p (kc ri c) -> p kc ri c", kc=2, ri=2), [ps.r], [dst.r])
            for q4 in range(4):
                P.dma(Yd_d[q4 * 1024:(q4 + 1) * 1024, :].rearrange("(s p) v -> p s v", p=128), Yl.t[:, q4 * 8:(q4 + 1) * 8, :], reads=[Yl.r], writes=[R_Yd])
            yf = [mk(ph, "yf%d" % kc, [128, NT], BF16) for kc in range(2)]
            for kc in range(2):
                ps = psum()
                k4 = 0
                for tch in range(2):
                    for (ri, tab) in ((0, C256), (1, S256)):
                        mm(ps, ps.t[:, :256], Yc.t[:, tch, ri * 256 + kc * 128: ri * 256 + (kc + 1) * 128], tab[:, tch, :], k4 == 0, k4 == 3, [Yc.r, cb.r])
                        k4 += 1
                evac(yf[kc].t[:, 0:TC], ps.t[:, :256], [ps.r], [yf[kc].r])
            L1 = mk(ph, "L1", [128, 64, 256], BF16)
            for ri in range(2):
                P.dma(L1.t[ri * 64:(ri + 1) * 64, :, :],
                      Yd_d[:, ri * 256:(ri + 1) * 256].rearrange("(t1 t2) c -> t1 t2 c", t2=64), reads=[R_Yd], writes=[L1.r])
            A_sb = mk(ph, "A_sb", [128, 64 * 256], BF16)
            L1f = L1.t[:].rearrange("p a b -> p (a b)")
            for i in range(32):
                ps = psum()
                mm(ps, ps.t[:, :512], W1, L1f[:, i * 512:(i + 1) * 512], True, True, [cb.r, L1.r])
                evac(A_sb.t[:, i * 512:(i + 1) * 512], ps.t[:, :512], [ps.r], [A_sb.r])
            P.dma(Ad_d, A_sb.t[:], reads=[A_sb.r], writes=[R_Ad])
            L3 = mk(ph, "L3", [128, 64, 256], BF16)
            m3 = mk(ph, "m3", [128, 4096], BF16)
            P.dma(m3.t[:], m3_d, writes=[m3.r])
            M3 = m3.t[:].rearrange("p (a b) -> p a b", a=64)
            for ri in range(2):
                P.dma(L3.t[ri * 64:(ri + 1) * 64, :, :],
                      Ad_d[ri * 64:(ri + 1) * 64, :].rearrange("f1 (t2 c) -> t2 f1 c", c=256), reads=[R_Ad], writes=[L3.r])
            for kc in range(2):
                yv = yf[kc].t[:, TC:NT].rearrange("p (f2 f1) -> p f1 f2", f1=64)
                for g8 in range(8):
                    ps = psum()
                    for i in range(8):
                        f1 = g8 * 8 + i
                        mmx(ps, ps.t[:, i * 64:(i + 1) * 64], L3.t[:, f1, kc * 128:(kc + 1) * 128], M3[:, f1, :], True, True, [L3.r, m3.r], i == 0)
                    evac(yv[:, g8 * 8:(g8 + 1) * 8, :], ps.t[:, :512].rearrange("p (a b) -> p a b", b=64), [ps.r], [yf[kc].r])
                P.dma(yT_d[512 + kc * 128:512 + (kc + 1) * 128, :], yf[kc].t[:], reads=[yf[kc].r], writes=[R_y])
            P.flush()
        P.barrier()
        if stop == ("B3", l): return finish()

        ffn = contextlib.ExitStack()
        Wg = mk(ffn, "Wg", [128, 8, DFF], BF16, True); Wu = mk(ffn, "Wu", [128, 8, DFF], BF16, True)
        with contextlib.ExitStack() as ph:
            Wo = mk(ph, "Wo", [128, 8, D], BF16)
            wl = WLoader(ph, 1024, ("act", "dve"))
            for kc in range(8):
                wl.load(Wo.t[:, kc, :], Wo.r, w_out_d[l, kc * 128:(kc + 1) * 128, :], D)
            pre = []
            for kc in range(8):
                for (W_, wd_) in ((Wg, w_g_d), (Wu, w_u_d)):
                    for c0 in (0, 1024, 2048):
                        n_ = min(1024, DFF - c0)
                        pre.append((W_.t[:, kc, c0:c0 + n_], W_.r, wd_[l, kc * 128:(kc + 1) * 128, c0:c0 + n_], n_))
            xs = [mk(ph, "xs%d" % i, [128, 8, 512]) for i in range(2)]
            ys = [mk(ph, "ys%d" % i, [128, 8, 512], BF16) for i in range(2)]
            mt = mk(ph, "mt", [128, 8, 512]); sq = mk(ph, "sq", [128, 8, 512], BF16)
            lnv = mk(ph, "lnv", [128, 512]); rstd = mk(ph, "rstd", [128, 512])
            tmp = [mk(ph, "tmp%d" % i, [128, 512]) for i in range(2)]
            def load_c(ti):
                t0, N = TT[ti]
                P.dma(xs[ti % 2].t[:, :, :N], src_d[:, t0:t0 + N].rearrange("(c p) t -> p c t", p=128), reads=src_reads, writes=[xs[ti % 2].r])
                P.dma(ys[ti % 2].t[:, :, :N], yT_d[:, t0:t0 + N].rearrange("(c p) t -> p c t", p=128), reads=[R_y], writes=[ys[ti % 2].r])
            load_c(0)
            for ti, (t0, N) in enumerate(TT):
                j = 1 if ti == 0 else 0
                if ti + 1 < len(TT): load_c(ti + 1)
                x_ = xs[ti % 2]; y_ = ys[ti % 2]
                for dc in range(8):
                    ps = psum()
                    for kc in range(8):
                        mm(ps, ps.t[:, :N], Wo.t[:, kc, dc * 128:(dc + 1) * 128], y_.t[:, kc, :N], kc == 0, kc == 7, [Wo.r, y_.r])
                    cp("act", mt.t[:, dc, :N], ps.t[:, :N], [ps.r], [mt.r])
                act(sq.t[:, :, :N], mt.t[:, :, :N], AF.Square, [mt.r], [sq.r])
                ps = psum()
                for c in range(8):
                    mm(ps, ps.t[:, :N], ones_b, sq.t[:, c, :N], c == 0, c == 7, [cb.r, sq.r])
                rstd_from_ps(ps, N, lnv, rstd, 1.0 / D)
                for c in range(8):
                    tm = tmp[c % 2]
                    tt("dve", tm.t[:, :N], mt.t[:, c, :N], rstd.t[:, :N], ALU.mult, [mt.r, rstd.r], [tm.r])
                    stt("dve", x_.t[:, c, :N], tm.t[:, :N], pv(G1, l, c, j), x_.t[:, c, :N], ALU.mult, ALU.add, [tm.r, G1.r, x_.r], [x_.r])
                P.dma(res_d[:, t0:t0 + N].rearrange("(c p) t -> p c t", p=128), x_.t[:, :, :N], reads=[x_.r] + src_reads, writes=[R_res])
                for _ in range(6):
                    if pre: wl.piece(*pre.pop(0), eng="pool")
            while pre: wl.piece(*pre.pop(0), eng="pool")
            P.flush()
        P.barrier()
        if stop == ("C", l):
            ffn.close(); return finish()

        with contextlib.ExitStack() as ph:
            Wd = mk(ph, "Wd", [128, 22, D], BF16)
            wl = WLoader(ph, 1024, ("act", "dve"))
            preD = [(Wd.t[:, kc, :], Wd.r, w_d_d[l, kc * 128:(kc + 1) * 128, :], D) for kc in range(22)]
            NB = 256
            xs = [mk(ph, "xs%d" % i, [128, 8, NB]) for i in range(2)]
            sq = mk(ph, "sq", [128, 8, NB], BF16); h = mk(ph, "h", [128, 8, NB], BF16)
            hid = mk(ph, "hid", [128, 22, NB], BF16); mt = mk(ph, "mt", [128, 8, NB])
            lnv = mk(ph, "lnv", [128, NB]); rstd = mk(ph, "rstd", [128, NB])
            tmp = [mk(ph, "tmp%d" % i, [128, NB]) for i in range(2)]
            sg = [mk(ph, "sg%d" % i, [128, NB]) for i in range(2)]
            last = (l == n_layers - 1)
            def load_d(ti):
                t0, N = TT256[ti]
                P.dma(xs[ti % 2].t[:], res_d[:, t0:t0 + N].rearrange("(c p) t -> p c t", p=128), reads=[R_res], writes=[xs[ti % 2].r])
            load_d(0)
            for ti, (t0, N) in enumerate(TT256):
                j = 1 if ti == 0 else 0
                if ti + 1 < len(TT256): load_d(ti + 1)
                x_ = xs[ti % 2]
                tt("pool", sq.t[:], x_.t[:], x_.t[:], ALU.mult, [x_.r], [sq.r])
                ps = psum()
                for c in range(8):
                    mm(ps, ps.t[:, :N], ones_b, sq.t[:, c, :], c == 0, c == 7, [cb.r, sq.r])
                rstd_from_ps(ps, N, lnv, rstd, 1.0 / D)
                for c in range(8):
                    tm = tmp[c % 2]
                    tt("dve", tm.t[:], x_.t[:, c, :], rstd.t[:], ALU.mult, [x_.r, rstd.r], [tm.r])
                    act(h.t[:, c, :], tm.t[:], AF.Identity, [tm.r, A2.r, mod.r], [h.r], scale=pv(A2, l, c, j), bias=modv(l, 3, c, j))
                for fc in range(22):
                    pg = psum(); pu = psum()
                    for kc in range(8):
                        mm(pg, pg.t[:, :N], Wg.t[:, kc, fc * 128:(fc + 1) * 128], h.t[:, kc, :], kc == 0, kc == 7, [Wg.r, h.r])
                    for kc in range(8):
                        mm(pu, pu.t[:, :N], Wu.t[:, kc, fc * 128:(fc + 1) * 128], h.t[:, kc, :], kc == 0, kc == 7, [Wu.r, h.r])
                    s_ = sg[fc % 2]
                    act(s_.t[:], pg.t[:, :N], AF.Silu, [pg.r], [s_.r])
                    tt("dve", hid.t[:, fc, :], s_.t[:], pu.t[:, :N], ALU.mult, [s_.r, pu.r], [hid.r])
                    if preD: wl.piece(*preD.pop(0))
                for dc in range(8):
                    ps = psum()
                    for kc in range(22):
                        mm(ps, ps.t[:, :N], Wd.t[:, kc, dc * 128:(dc + 1) * 128], hid.t[:, kc, :], kc == 0, kc == 21, [Wd.r, hid.r])
                    cp("act", mt.t[:, dc, :], ps.t[:, :N], [ps.r], [mt.r])
                tt("pool", sq.t[:], mt.t[:], mt.t[:], ALU.mult, [mt.r], [sq.r])
                ps = psum()
                for c in range(8):
                    mm(ps, ps.t[:, :N], ones_b, sq.t[:, c, :], c == 0, c == 7, [cb.r, sq.r])
                rstd_from_ps(ps, N, lnv, rstd, 1.0 / D)
                for c in range(8):
                    tm = tmp[c % 2]
                    tt("dve", tm.t[:], mt.t[:, c, :], rstd.t[:], ALU.mult, [mt.r, rstd.r], [tm.r])
                    stt("dve", x_.t[:, c, :], tm.t[:], pv(G2, l, c, j), x_.t[:, c, :], ALU.mult, ALU.add, [tm.r, G2.r, x_.r], [x_.r])
                if last and ti >= 1:
                    P.dma(out_d[:, t0 - TC:t0 - TC + N].rearrange("(c p) t -> p c t", p=128), x_.t[:], reads=[x_.r])
                else:
                    P.dma(res_d[:, t0:t0 + N].rearrange("(c p) t -> p c t", p=128), x_.t[:], reads=[x_.r, R_res], writes=[R_res])
            P.flush()
        ffn.close()
        P.barrier()
        if stop == ("D", l): return finish()

    return finish()


def _bf(a):
    return np.ascontiguousarray(a.astype(np.float32)).astype(ml_dtypes.bfloat16)

def _consts():
    ones = np.ones((128, 128), np.float32)
    ident = np.eye(128, dtype=np.float32)
    c = np.arange(128)
    same = (c[:, None] // 64) == (c[None, :] // 64)
    ang = 2 * np.pi * ((c[:, None] % 64) * (c[None, :] % 64)) / 64.0
    CDm = np.concatenate([np.where(same, np.cos(ang), 0.0), np.where(same, -np.sin(ang), 0.0)], axis=1)
    t1 = np.arange(64)
    a64 = 2 * np.pi * np.outer(t1, t1) / 64.0
    Cw, Sw = np.cos(a64), np.sin(a64)
    W1 = np.block([[Cw, -Sw], [Sw, Cw]])
    t2 = np.arange(64)[:, None, None]; f1 = np.arange(64)[None, :, None]; f2 = np.arange(64)[None, None, :]
    th = 2 * np.pi * (t2 * f1 / 4096.0 + t2 * f2 / 64.0)
    sc = 1.0 / np.sqrt(4096.0 * 64.0)
    M3 = np.concatenate([np.cos(th), np.sin(th)], axis=0) * sc
    t = np.arange(256)
    a256 = 2 * np.pi * np.outer(t, t) / 256.0
    sc2 = 1.0 / np.sqrt(256.0 * 64.0)
    C2 = (np.cos(a256) * sc2).reshape(2, 128, 256).transpose(1, 0, 2).reshape(128, 512)
    S2 = (np.sin(a256) * sc2).reshape(2, 128, 256).transpose(1, 0, 2).reshape(128, 512)
    cbf = np.concatenate([ones, ident, CDm, W1, C2, S2], axis=1)
    j = np.arange(128)[:, None]; i = np.arange(128)[None, :]
    cf = np.concatenate([(i >= j), (i <= j)], axis=1).astype(np.float32)
    return _bf(cbf), np.ascontiguousarray(cf), _bf(M3.reshape(128, 4096))

def _smalls(inp):
    sm = np.zeros((128, NS), np.float32)
    def put(name, arr):
        arr = np.asarray(arr, np.float32)
        sm[:, _off[name]:_off[name] + arr.shape[1]] = arr
    put("b_ada", np.asarray(inp["b_ada"]).reshape(L, 6, 8, 128).transpose(3, 0, 1, 2).reshape(128, L * 48))
    gains = np.stack([inp["g_pre_mix"], inp["g_post_mix"], inp["g_pre_ffn"], inp["g_post_ffn"]], axis=1)
    put("gain", gains.reshape(L, 4, 8, 128).transpose(3, 0, 1, 2).reshape(128, L * 32))
    put("g_gla", np.asarray(inp["g_gla"]).T)
    put("w_conv", np.asarray(inp["w_conv"]).reshape(L, 4, 2, 128).transpose(3, 0, 2, 1).reshape(128, L * 8))
    put("b_conv", np.asarray(inp["b_conv"]).reshape(L, 2, 128).transpose(2, 0, 1).reshape(128, L * 2))
    for nm in ("b_rg_a", "b_rg_x"):
        put(nm, np.asarray(inp[nm]).reshape(L, 2, 2, 128).transpose(3, 0, 1, 2).reshape(128, L * 4))
    put("lam", np.asarray(inp["rg_lam"]).reshape(L, 2, 2, 128).transpose(3, 0, 1, 2).reshape(128, L * 4))
    put("b_dec", np.asarray(inp["b_dec"]).reshape(L, 2, 2, 128).transpose(3, 0, 1, 2).reshape(128, L * 4))
    return sm

def _wrg(inp):
    w = np.zeros((L, 2, 2, 2, 128, 128), np.float32)
    for gi, nm in enumerate(("w_rg_a", "w_rg_x")):
        a = np.asarray(inp[nm], np.float32)
        for ch in range(2):
            for hh in range(2):
                w[:, :, gi, ch, hh * 64:(hh + 1) * 64, hh * 64:(hh + 1) * 64] = a[:, :, 2 * ch + hh]
    return w

def make_in_maps(inp):
    cbf, cf, m3 = _consts()
    sm = _smalls(inp)
    wrg = _wrg(inp)
    f = lambda k: np.ascontiguousarray(np.asarray(inp[k], np.float32))
    shared = {"smalls": sm, "w_ada": f("w_ada"), "w_in": f("w_in"), "w_out": f("w_out"), "w_ffn_gate": f("w_ffn_gate"),
              "w_ffn_up": f("w_ffn_up"), "w_ffn_down": f("w_ffn_down"), "w_dec": f("w_dec"), "wrg": wrg, "cbf": cbf, "cf32": cf, "m3": m3}
    x = np.asarray(inp["x"], np.float32); ctx = np.asarray(inp["ctx"], np.float32)
    c = np.asarray(inp["c"], np.float32); c_ctx = np.asarray(inp["c_ctx"], np.float32)
    maps = []
    for b in range(8):
        xT = np.ascontiguousarray(np.concatenate([ctx[b].T, x[b].T], axis=1))
        cc = np.stack([c[b], c_ctx], axis=1).reshape(8, 128, 2).transpose(1, 0, 2).reshape(128, 16)
        m = dict(shared); m["xT"] = xT; m["cc"] = np.ascontiguousarray(cc)
        maps.append(m)
    return maps

_NC = {}
def kernel(**inputs):
    if "nc" not in _NC:
        _NC["nc"] = build()
    maps = make_in_maps(inputs)
    res = run_bass_kernel_spmd(_NC["nc"], maps, core_ids=list(range(8)))
    out = np.stack([np.asarray(res.results[b]["outT"], np.float32).T for b in range(8)], axis=0)
    return np.ascontiguousarray(out)
```
